# Optimizing a Trainium2 kernel written in Bass

```python
import math
import jax, jax.numpy as jnp
from jax import lax
import numpy as np

D_MODEL = 1024
BATCH = 4
SEQ = 8192
DEPTH = 2

D_MIX = D_MODEL
W_CONF = D_MIX // 4
W_POOL = D_MIX // 4
W_SCONV = D_MIX // 4
W_ATTN = D_MIX - W_CONF - W_POOL - W_SCONV
CONF_WIDTH = 31
POOL_WINDOWS = (2, 4, 8, 16)
POOL_GROUP = W_POOL // len(POOL_WINDOWS)
SCONV_WIDTH = 3
ATTN_HEAD_DIM = 64
ATTN_HEADS = W_ATTN // ATTN_HEAD_DIM
ATTN_BLOCK = 128
SPLIT_SIZES = (W_CONF, W_CONF, W_POOL, W_SCONV, W_SCONV, W_SCONV, W_ATTN, W_ATTN, W_ATTN)
D_IN = sum(SPLIT_SIZES)
PEER_HEADS = 8
PEER_NKEYS = 128
PEER_EXPERTS = PEER_NKEYS * PEER_NKEYS
PEER_DKEY = 256
PEER_TOPK = 16
PEER_CHUNK = 128
DEEPNORM_ALPHA = (2 * DEPTH) ** 0.25
DEEPNORM_BETA = (8 * DEPTH) ** -0.25
LN_EPS = 1e-5

kernel_name = "hybrid_headgroup_peer_deepnorm"


def layer_norm(x, g, b):
    xf = x.astype(jnp.float32)
    mu = jnp.mean(xf, axis=-1, keepdims=True)
    xc = xf - mu
    var = jnp.mean(xc * xc, axis=-1, keepdims=True)
    return (xc * lax.rsqrt(var + LN_EPS) * g + b).astype(x.dtype)


def depthwise_causal_conv(x, w):
    k_width, ch = w.shape
    return lax.conv_general_dilated(
        x, w[:, None, :].astype(x.dtype), window_strides=(1,), padding=[(k_width - 1, 0)],
        dimension_numbers=("NWC", "WIO", "NWC"), feature_group_count=ch)


def conformer_conv(a_val, a_gate, conv_w, conv_b, norm_g, norm_b):
    u = a_val * jax.nn.sigmoid(a_gate)
    h = depthwise_causal_conv(u, conv_w) + conv_b
    return jax.nn.silu(layer_norm(h, norm_g, norm_b))


def multiscale_pool(p, pool_w, pool_scale):
    bn, s, _ = p.shape
    groups = p.reshape(bn, s, len(POOL_WINDOWS), POOL_GROUP)
    t = jnp.arange(s)
    outs = []
    for g, win in enumerate(POOL_WINDOWS):
        xg = groups[:, :, g, :].astype(jnp.float32)
        cs = jnp.cumsum(xg, axis=1)
        prev = jnp.pad(cs, ((0, 0), (win, 0), (0, 0)))[:, :s]
        count = jnp.minimum(t + 1, win).astype(jnp.float32)
        outs.append((cs - prev) / count[None, :, None] - xg)
    pooled = jnp.stack(outs, axis=2).astype(p.dtype)
    mixed = jnp.einsum("bsgc,gce->bsge", pooled, pool_w)
    return mixed.reshape(bn, s, W_POOL) * pool_scale


def stick_breaking_attention(q, k, v):
    bn, s, _ = q.shape
    nb = s // ATTN_BLOCK

    def heads(a):
        return a.reshape(bn, s, ATTN_HEADS, ATTN_HEAD_DIM).transpose(0, 2, 1, 3)

    q, k, v = heads(q), heads(k), heads(v)
    q_blocks = q.reshape(bn, ATTN_HEADS, nb, ATTN_BLOCK, ATTN_HEAD_DIM).transpose(2, 0, 1, 3, 4)
    k_pos = jnp.arange(s)
    scale = ATTN_HEAD_DIM ** -0.5

    def block(args):
        q_blk, blk = args
        z = jnp.einsum("bhqd,bhkd->bhqk", q_blk, k).astype(jnp.float32) * scale
        q_pos = blk * ATTN_BLOCK + jnp.arange(ATTN_BLOCK)
        causal = k_pos[None, :] < q_pos[:, None]
        log_stay = jnp.where(causal, jax.nn.log_sigmoid(-z), 0.0)
        cum = jnp.cumsum(log_stay, axis=-1)
        log_w = jax.nn.log_sigmoid(z) + (cum[..., -1:] - cum)
        w = jnp.where(causal, jnp.exp(log_w), 0.0)
        return jnp.einsum("bhqk,bhkd->bhqd", w.astype(v.dtype), v)

    out = lax.map(block, (q_blocks, jnp.arange(nb)))
    return out.transpose(1, 0, 3, 2, 4).reshape(bn, s, W_ATTN)


def token_mix(x, w_in, conv_a_w, conv_a_b, norm_a_g, norm_a_b, pool_w, pool_scale, conv_c_w, w_out):
    proj = jnp.einsum("bsd,de->bse", x, w_in)
    split_points = [int(i) for i in np.cumsum(SPLIT_SIZES)[:-1]]
    a_val, a_gate, p_in, c_h, c_gate_b, c_gate_c, q, k, v = jnp.split(proj, split_points, axis=-1)
    y_a = conformer_conv(a_val, a_gate, conv_a_w, conv_a_b, norm_a_g, norm_a_b)
    y_b = multiscale_pool(p_in, pool_w, pool_scale)
    y_c = c_gate_b * depthwise_causal_conv(c_gate_c * c_h, conv_c_w)
    y_d = stick_breaking_attention(q, k, v)
    y = jnp.concatenate([y_a, y_b, y_c, y_d], axis=-1)
    return jnp.einsum("bse,ed->bsd", y, w_out)


def peer_ffn(x, wq, sub_keys, u_tab, v_tab):
    bn, s, d = x.shape
    t = bn * s
    xf = x.reshape(t, d)
    q = (xf @ wq).reshape(t, PEER_HEADS, 2, PEER_DKEY // 2)
    scores = jnp.einsum("thpc,hpnc->thpn", q, sub_keys).astype(jnp.float32)
    s_top, i_top = lax.top_k(scores, PEER_TOPK)
    cand_s = (s_top[:, :, 0, :, None] + s_top[:, :, 1, None, :]).reshape(t, PEER_HEADS, PEER_TOPK * PEER_TOPK)
    cand_i = (i_top[:, :, 0, :, None] * PEER_NKEYS + i_top[:, :, 1, None, :]).reshape(t, PEER_HEADS, PEER_TOPK * PEER_TOPK)
    best_s, best_pos = lax.top_k(cand_s, PEER_TOPK)
    expert_idx = jnp.take_along_axis(cand_i, best_pos, axis=-1)
    gates = jax.nn.softmax(best_s, axis=-1)
    n_chunks = t // PEER_CHUNK

    def chunk(args):
        xc, ec, gc = args
        u = jnp.take(u_tab, ec, axis=0)
        act = jax.nn.gelu(jnp.einsum("cd,chkd->chk", xc, u).astype(jnp.float32)) * gc
        vv = jnp.take(v_tab, ec, axis=0)
        return jnp.einsum("chk,chkd->cd", act.astype(vv.dtype), vv)

    out = lax.map(chunk, (xf.reshape(n_chunks, PEER_CHUNK, d),
                          expert_idx.reshape(n_chunks, PEER_CHUNK, PEER_HEADS, PEER_TOPK),
                          gates.reshape(n_chunks, PEER_CHUNK, PEER_HEADS, PEER_TOPK)))
    return out.reshape(bn, s, d).astype(x.dtype)


def setup_inputs(seed: int = 0) -> dict:
    key = jax.random.key(seed)
    ks = jax.random.split(key, 20)
    L = DEPTH

    def nrm(k, shape, scale):
        return jax.random.normal(k, shape, jnp.float32) * scale

    return {
        "x": nrm(ks[0], (BATCH, SEQ, D_MODEL), 1.0),
        "w_in": nrm(ks[1], (L, D_MODEL, D_IN), D_MODEL ** -0.5),
        "conv_a_w": nrm(ks[2], (L, CONF_WIDTH, W_CONF), CONF_WIDTH ** -0.5),
        "conv_a_b": nrm(ks[3], (L, W_CONF), 0.02),
        "norm_a_g": 1.0 + nrm(ks[4], (L, W_CONF), 0.02),
        "norm_a_b": nrm(ks[5], (L, W_CONF), 0.02),
        "pool_w": nrm(ks[6], (L, len(POOL_WINDOWS), POOL_GROUP, POOL_GROUP), POOL_GROUP ** -0.5),
        "pool_scale": 1.0 + nrm(ks[7], (L, W_POOL), 0.02),
        "conv_c_w": nrm(ks[8], (L, SCONV_WIDTH, W_SCONV), SCONV_WIDTH ** -0.5),
        "w_out": nrm(ks[9], (L, D_MIX, D_MODEL), D_MIX ** -0.5 * DEEPNORM_BETA),
        "ln1_g": 1.0 + nrm(ks[10], (L, D_MODEL), 0.02),
        "ln1_b": nrm(ks[11], (L, D_MODEL), 0.02),
        "peer_wq": nrm(ks[12], (L, D_MODEL, PEER_HEADS * PEER_DKEY), D_MODEL ** -0.5),
        "peer_keys": nrm(ks[13], (L, PEER_HEADS, 2, PEER_NKEYS, PEER_DKEY // 2), (PEER_DKEY // 2) ** -0.5),
        "peer_u": nrm(ks[14], (L, PEER_EXPERTS, D_MODEL), D_MODEL ** -0.5),
        "peer_v": nrm(ks[15], (L, PEER_EXPERTS, D_MODEL), PEER_HEADS ** -0.5 * DEEPNORM_BETA),
        "ln2_g": 1.0 + nrm(ks[16], (L, D_MODEL), 0.02),
        "ln2_b": nrm(ks[17], (L, D_MODEL), 0.02),
    }


def reference(x, w_in, conv_a_w, conv_a_b, norm_a_g, norm_a_b, pool_w, pool_scale, conv_c_w, w_out,
              ln1_g, ln1_b, peer_wq, peer_keys, peer_u, peer_v, ln2_g, ln2_b):
    for l in range(DEPTH):
        m = token_mix(x, w_in[l], conv_a_w[l], conv_a_b[l], norm_a_g[l], norm_a_b[l],
                      pool_w[l], pool_scale[l], conv_c_w[l], w_out[l])
        x = layer_norm(DEEPNORM_ALPHA * x + m, ln1_g[l], ln1_b[l])
        f = peer_ffn(x, peer_wq[l], peer_keys[l], peer_u[l], peer_v[l])
        x = layer_norm(DEEPNORM_ALPHA * x + f, ln2_g[l], ln2_b[l])
    return x
```

```python
import numpy as np
import concourse.bass as bass
import concourse.mybir as mybir
from concourse.bass_utils import run_bass_kernel_spmd

F32 = mybir.dt.float32
BF16 = mybir.dt.bfloat16
U32 = mybir.dt.uint32
I32 = mybir.dt.int32
AF = mybir.ActivationFunctionType
ALU = mybir.AluOpType


class Res:
    def __init__(self, name, t=None):
        self.name = name
        self.t = t
        self.w = None
        self.r = {}
        self.dsem = None
        self.dcount = 0


class Queue:
    def __init__(self, fw, eng, name, is_pe=False):
        self.eng = eng
        self.name = name
        self.sem = fw.new_sem("q_" + name)
        self.n = 0
        self.seen = {}
        self.prog = []
        self.is_pe = is_pe
        self.hist = []
        self.absorbed = {}


class FW:
    def __init__(self, nc):
        self.nc = nc
        self.nsem = 0
        self.pe = Queue(self, nc.tensor, "pe", True)
        self.act = Queue(self, nc.scalar, "act")
        self.dve = Queue(self, nc.vector, "dve")
        self.pool = Queue(self, nc.gpsimd, "pool")
        self.sync = Queue(self, nc.sync, "sync")
        self.queues = [self.pe, self.act, self.dve, self.pool, self.sync]
        self.finals = []
        self.ninstr = 0
        self.dres = []
        self.sem_pool = []
        self.sem_pool_sw = []

    def new_sem(self, name):
        self.nsem += 1
        return self.nc.alloc_semaphore("s%d_%s" % (self.nsem, name))

    def sb(self, name, shape, dt):
        return Res(name, self.nc.alloc_sbuf_tensor("g_" + name, list(shape), dt))

    def ps(self, name, shape, dt=F32):
        return Res(name, self.nc.alloc_psum_tensor(name, list(shape), dt))

    def res(self, name):
        return Res(name)

    def _wait(self, q, ev):
        q2, sem, val = ev
        k = id(sem)
        if q.seen.get(k, 0) >= val:
            return
        q.seen[k] = val
        q.hist.append((q.n, k, val))
        q.prog.append(lambda: q.eng.wait_ge(sem, val))
        if isinstance(q2, Queue) and q2 is not q:
            pos = q.absorbed.get(id(q2), 0)
            h = q2.hist
            while pos < len(h) and h[pos][0] < val:
                _, k2, v2 = h[pos]
                if q.seen.get(k2, 0) < v2:
                    q.seen[k2] = v2
                    q.hist.append((q.n, k2, v2))
                pos += 1
            q.absorbed[id(q2)] = pos

    def _deps(self, q, reads, writes, is_dma):
        for r in reads:
            if r.w is not None:
                if is_dma or not (r.w[0] is q and q.is_pe):
                    self._wait(q, r.w)
        for w in writes:
            if w.w is not None and (is_dma or not (w.w[0] is q and q.is_pe)):
                self._wait(q, w.w)
            for ev in w.r.values():
                if is_dma or not (ev[0] is q and q.is_pe):
                    self._wait(q, ev)

    def op(self, q, fn, reads, writes):
        self._deps(q, reads, writes, False)
        q.n += 1
        n = q.n
        sem = q.sem
        q.prog.append(lambda: fn().then_inc(sem, 1))
        ev = (q, sem, n)
        for r in reads:
            r.r[id(q)] = ev
        for w in writes:
            w.w = ev
            w.r = {}
        self.ninstr += 1

    def dma(self, q, out_ap, in_ap, reads, writes, semres=None, final=False, **kw):
        if semres is None:
            semres = (writes + reads)[0]
        if semres.dsem is None:
            pool = self.sem_pool_sw if q is self.pool else self.sem_pool
            if pool:
                semres.dsem, semres.dcount = pool.pop()
            else:
                semres.dsem = self.new_sem("d_" + semres.name)
            semres.dsw = (q is self.pool)
            self.dres.append(semres)
        assert semres.dsw == (q is self.pool), semres.name
        self._deps(q, reads, writes, True)
        if semres.dcount:
            self._wait(q, (None, semres.dsem, semres.dcount))
        semres.dcount += 16
        sem, val = semres.dsem, semres.dcount
        eng = q.eng
        cv = lambda a: a() if callable(a) else a
        q.prog.append(lambda: eng.dma_start(out=cv(out_ap), in_=cv(in_ap), **kw).then_inc(sem, 16))
        ev = (None, sem, val)
        for r in reads:
            r.r[("d", id(sem))] = ev
        for w in writes:
            w.w = ev
            w.r = {}
        if final:
            self.finals.append(ev)
        self.ninstr += 1

    def emit(self):
        for ev in self.finals:
            self._wait(self.sync, ev)
        nc = self.nc
        with nc.Block() as block:
            @block.tensor
            def _(e):
                for f in self.pe.prog:
                    f()

            @block.scalar
            def _(e):
                for f in self.act.prog:
                    f()

            @block.vector
            def _(e):
                for f in self.dve.prog:
                    f()

            @block.gpsimd
            def _(e):
                for f in self.pool.prog:
                    f()

            @block.sync
            def _(e):
                for f in self.sync.prog:
                    f()

    def barrier(self):
        sems = []
        for q in self.queues:
            sems.append((q, q.sem, q.n))
        for ev in self.dma_events():
            sems.append(ev)
        for q in self.queues:
            for ev in sems:
                if ev[0] is q or ev[2] == 0:
                    continue
                self._wait(q, ev)

    def phase_begin(self):
        return len(self.dres)

    def phase_end(self, idx):
        keep = []
        for r in self.dres[idx:]:
            if getattr(r, "persist", False):
                keep.append(r)
            else:
                (self.sem_pool_sw if r.dsw else self.sem_pool).append((r.dsem, r.dcount))
                r.dsem = None
        self.dres = self.dres[:idx] + keep

    def dma_events(self):
        return [(None, r.dsem, r.dcount) for r in self.dres if r.dcount]

    def sbx(self, stack, name, shape, dt):
        self.nsb = getattr(self, "nsb", 0) + 1
        t = stack.enter_context(self.nc.sbuf_tensor("%s_%d" % (name, self.nsb), list(shape), dt))
        return Res(name, t)


D = 1024
DIN = 1920
NPC = 76
ALPHA = float((2 * 2) ** 0.25)
LN_EPS = 1e-5
TS = 512
NEG = -1.0e30
C_IDENT = 0
C_NEGTRI = 128
C_NEGONES = 256
C_ONESDIV = 384
C_MASK = 512
C_INVC0 = C_MASK + 2048
C_INVW = C_INVC0 + 32
C_IOTA = C_INVW + 2
C_IOTA16 = C_IOTA + 128
NCONST = C_IOTA16 + 16


def make_consts():
    c = np.zeros((128, NCONST), np.float32)
    c[:, C_IDENT:C_IDENT + 128] = np.eye(128, dtype=np.float32)
    j = np.arange(128)[:, None]
    s = np.arange(128)[None, :]
    c[:, C_NEGTRI:C_NEGTRI + 128] = -(j >= s).astype(np.float32)
    c[:, C_NEGONES:C_NEGONES + 128] = -1.0
    c[:, C_ONESDIV:C_ONESDIV + 128] = 1.0 / 256.0
    t = np.arange(512)[None, :]
    for d in range(4):
        c[:, C_MASK + d * 512:C_MASK + (d + 1) * 512] = ((128 * d + j) < t).astype(np.float32)
    wins = (2, 4, 8, 16)
    for c2 in range(2):
        for p in range(128):
            win = wins[2 * c2 + p // 64]
            c[p, C_INVC0 + c2 * 16:C_INVC0 + (c2 + 1) * 16] = 1.0 / np.minimum(np.arange(16) + 1, win)
            c[p, C_INVW + c2] = 1.0 / win
    c[:, C_IOTA:C_IOTA + 128] = np.arange(128, dtype=np.float32)[None, :]
    c[:, C_IOTA16:C_IOTA16 + 16] = np.arange(16, dtype=np.float32)[None, :]
    return c


class CX:
    pass


def _ops(cx):
    fw, nc = cx.fw, cx.nc

    def PE(fn, r, w):
        fw.op(fw.pe, fn, r, w)

    def ACT(fn, r, w):
        fw.op(fw.act, fn, r, w)

    def DVE(fn, r, w):
        fw.op(fw.dve, fn, r, w)

    def POOL(fn, r, w):
        fw.op(fw.pool, fn, r, w)

    def nb():
        cx.bi = (cx.bi + 1) % len(cx.rot)
        return cx.rot[cx.bi]

    return PE, ACT, DVE, POOL, nb


def cview(cx, off, n):
    return cx.cst.t[:, off:off + n]


def load_cast(cx, dst, dst_ap_fn, src_ap, nk, ncols, step):
    fw = cx.fw
    v = src_ap.rearrange("(k p) n -> p k n", p=128)
    for k in range(nk):
        for c0 in range(0, ncols, step):
            c1 = min(ncols, c0 + step)
            fw.dma(fw.pool, dst_ap_fn(k, c0, c1), v[:, k, c0:c1], reads=[], writes=[dst])


def phase_m1(cx, l, xsrc):
    from contextlib import ExitStack
    _pidx = cx.fw.phase_begin()
    fw, nc, S = cx.fw, cx.nc, cx.S
    PE, ACT, DVE, POOL, nb = _ops(cx)
    cx.rot = cx.banks
    cx.bi = 0
    NST = S // TS
    cst = cx.cst
    ident = cview(cx, C_IDENT, 128)
    with ExitStack() as st_:
        sb = lambda n, s, d: fw.sbx(st_, n, s, d)
        w16 = sb("w16", [128, 8, DIN], BF16)
        pcs = sb("pcs", [128, NPC], F32)
        diagA = sb("diagA", [128, 62, 128], BF16)
        diagC = sb("diagC", [128, 6, 128], BF16)
        pw16 = sb("pw16", [128, 2, 128], BF16)
        xin = [sb("xin%d" % i, [128, D], F32) for i in range(2)]
        xT = sb("xT", [128, 8, TS], BF16)
        u = sb("u", [128, 2, 30 + TS], BF16)
        p = sb("p", [128, 2, 15 + TS], F32)
        g = sb("g", [128, 2, 2 + TS], BF16)
        sA = sb("sA", [128, 2, 15 + TS], F32)
        sB = sb("sB", [128, 2, 15 + TS], F32)
        sg = sb("sg", [128, TS], F32)
        h = sb("h", [128, 2, TS], F32)
        hsq = sb("hsq", [128, 2, TS], F32)
        msq = sb("msq", [128, TS], F32)
        var = sb("var", [128, TS], F32)
        rstd = sb("rstd", [128, TS], F32)
        t1 = sb("t1", [128, TS], F32)
        t2 = sb("t2", [128, TS], F32)
        tmp16 = sb("tmp16", [128, 16], F32)
        pooled = sb("pooled", [128, 2, TS], BF16)
        yabc_b = [sb("yabc%d" % i, [128, 6, TS], BF16) for i in range(2)]
        qst_b = [sb("qst%d" % i, [128, TS], BF16) for i in range(2)]
        kst_b = [sb("kst%d" % i, [128, TS], BF16) for i in range(2)]
        vst_b = [sb("vst%d" % i, [128, 4, 128], BF16) for i in range(2)]
        chs = sb("chs", [128, TS], F32)
        gb = sb("gb", [128, TS], F32)
        epsb = sb("epsb", [128, 1], F32)

        fw.dma(fw.sync, pcs.t[:], cx.pc[l], reads=[], writes=[pcs])
        load_cast(cx, w16, lambda k, c0, c1: w16.t[:, k, c0:c1], cx.w_in[l], 8, DIN, DIN)
        fw.dma(fw.pool, pw16.t[:], cx.poolw[l], reads=[], writes=[pw16])
        for i in range(62):
            DVE(lambda i=i: nc.vector.tensor_scalar(diagA.t[:, i, :], ident, pcs.t[:, i:i + 1], None, ALU.mult),
                [cst, pcs], [diagA])
        for i in range(6):
            DVE(lambda i=i: nc.vector.tensor_scalar(diagC.t[:, i, :], ident, pcs.t[:, 70 + i:71 + i], None, ALU.mult),
                [cst, pcs], [diagC])
        POOL(lambda: nc.gpsimd.memset(u.t[:, :, 0:30], 0.0), [], [u])
        POOL(lambda: nc.gpsimd.memset(p.t[:, :, 0:15], 0.0), [], [p])
        POOL(lambda: nc.gpsimd.memset(g.t[:, :, 0:2], 0.0), [], [g])
        POOL(lambda: nc.gpsimd.memset(sA.t[:], 0.0), [], [sA])
        POOL(lambda: nc.gpsimd.memset(sB.t[:], 0.0), [], [sB])
        POOL(lambda: nc.gpsimd.memset(epsb.t[:], LN_EPS), [], [epsb])
        def proj_chunk(cidx):
            b = nb()
            for k in range(8):
                PE(lambda k=k, b=b: nc.tensor.matmul(b.t[:, :], w16.t[:, k, cidx * 128:(cidx + 1) * 128], xT.t[:, k, :],
                                                     start=(k == 0), stop=(k == 7)), [w16, xT], [b])
            return b

        cnt = 0
        for st in range(NST):
            t0 = st * TS
            yabc, qst, kst, vst = yabc_b[st % 2], qst_b[st % 2], kst_b[st % 2], vst_b[st % 2]
            for j in range(4):
                xi = xin[j % 2]
                fw.dma(fw.sync, xi.t[:], xsrc(st, j), reads=[], writes=[xi])
                for half in range(2):
                    b = nb()
                    for kk in range(4):
                        k = half * 4 + kk
                        PE(lambda b=b, kk=kk, k=k, xi=xi: nc.tensor.transpose(b.t[:, kk * 128:(kk + 1) * 128],
                                                                               xi.t[:, k * 128:(k + 1) * 128], ident),
                           [xi, cst], [b])
                    dst = xT.t[:, half * 4:(half + 1) * 4, j * 128:(j + 1) * 128]
                    src = b.t[:, :].rearrange("p (k n) -> p k n", k=4)
                    if (cnt % 2) == 0:
                        ACT(lambda dst=dst, src=src: nc.scalar.copy(dst, src), [b], [xT])
                    else:
                        DVE(lambda dst=dst, src=src: nc.vector.tensor_copy(dst, src), [b], [xT])
                    cnt += 1
            for c2 in range(2):
                bA = proj_chunk(0 + c2)
                bG = proj_chunk(2 + c2)
                ACT(lambda bG=bG: nc.scalar.activation(sg.t[:], bG.t[:, :], AF.Sigmoid), [bG], [sg])
                DVE(lambda bA=bA, c2=c2: nc.vector.tensor_tensor(u.t[:, c2, 30:30 + TS], bA.t[:, :], sg.t[:], ALU.mult),
                    [bA, sg], [u])
                bH = nb()
                for k in range(31):
                    PE(lambda k=k, bH=bH, c2=c2: nc.tensor.matmul(bH.t[:, :], diagA.t[:, c2 * 31 + k, :], u.t[:, c2, k:k + TS],
                                                                  start=(k == 0), stop=(k == 30)), [diagA, u], [bH])
                ACT(lambda bH=bH, c2=c2: nc.scalar.activation(h.t[:, c2, :], bH.t[:, :], AF.Identity,
                                                              bias=pcs.t[:, 62 + c2:63 + c2]), [bH, pcs], [h])
                ACT(lambda bH=bH, c2=c2: nc.scalar.activation(hsq.t[:, c2, :], bH.t[:, :], AF.Square,
                                                              bias=pcs.t[:, 62 + c2:63 + c2]), [bH, pcs], [hsq])
                POOL(lambda c2=c2: nc.gpsimd.tensor_copy(u.t[:, c2, 0:30], u.t[:, c2, TS:TS + 30]), [u], [u])
            for c2 in range(2):
                bP = proj_chunk(4 + c2)
                ACT(lambda bP=bP, c2=c2: nc.scalar.copy(p.t[:, c2, 15:15 + TS], bP.t[:, :]), [bP], [p])
                W = 15 + TS
                DVE(lambda c2=c2: nc.vector.tensor_tensor(sA.t[:, c2, 1:W], p.t[:, c2, 1:W], p.t[:, c2, 0:W - 1], ALU.add),
                    [p], [sA])
                DVE(lambda c2=c2: nc.vector.tensor_tensor(sB.t[:, c2, 3:W], sA.t[:, c2, 3:W], sA.t[:, c2, 1:W - 2], ALU.add),
                    [sA], [sB])
                if c2 == 1:
                    DVE(lambda: nc.vector.tensor_tensor(sA.t[:, 1, 7:W], sB.t[:, 1, 7:W], sB.t[:, 1, 3:W - 4], ALU.add),
                        [sB], [sA])
                    DVE(lambda: nc.vector.tensor_tensor(sB.t[:, 1, 15:W], sA.t[:, 1, 15:W], sA.t[:, 1, 7:W - 8], ALU.add),
                        [sA], [sB])
                for (lo, srcb) in ((0, sA), (64, sB)):
                    DVE(lambda lo=lo, srcb=srcb, c2=c2: nc.vector.scalar_tensor_tensor(
                        pooled.t[lo:lo + 64, c2, :], srcb.t[lo:lo + 64, c2, 15:W],
                        cst.t[lo:lo + 64, C_INVW + c2:C_INVW + c2 + 1], p.t[lo:lo + 64, c2, 15:W],
                        ALU.mult, ALU.subtract), [srcb, p, cst], [pooled])
                    if st == 0:
                        DVE(lambda lo=lo, srcb=srcb, c2=c2: nc.vector.tensor_tensor(
                            tmp16.t[lo:lo + 64, :], srcb.t[lo:lo + 64, c2, 15:31],
                            cst.t[lo:lo + 64, C_INVC0 + c2 * 16:C_INVC0 + (c2 + 1) * 16], ALU.mult), [srcb, cst], [tmp16])
                        DVE(lambda lo=lo, c2=c2: nc.vector.tensor_tensor(
                            pooled.t[lo:lo + 64, c2, 0:16], tmp16.t[lo:lo + 64, :], p.t[lo:lo + 64, c2, 15:31],
                            ALU.subtract), [tmp16, p], [pooled])
                bX = nb()
                PE(lambda bX=bX, c2=c2: nc.tensor.matmul(bX.t[:, :], pw16.t[:, c2, :], pooled.t[:, c2, :], start=True, stop=True),
                   [pw16, pooled], [bX])
                ACT(lambda bX=bX, c2=c2, yabc=yabc: nc.scalar.activation(yabc.t[:, 2 + c2, :], bX.t[:, :], AF.Copy,
                                                              scale=pcs.t[:, 68 + c2:69 + c2]), [bX, pcs], [yabc])
                POOL(lambda c2=c2: nc.gpsimd.tensor_copy(p.t[:, c2, 0:15], p.t[:, c2, TS:TS + 15]), [p], [p])
            for c2 in range(2):
                bCh = proj_chunk(6 + c2)
                bGb = proj_chunk(8 + c2)
                bGc = proj_chunk(10 + c2)
                ACT(lambda bCh=bCh: nc.scalar.copy(chs.t[:], bCh.t[:, :]), [bCh], [chs])
                DVE(lambda bGc=bGc, c2=c2: nc.vector.tensor_tensor(g.t[:, c2, 2:2 + TS], bGc.t[:, :], chs.t[:], ALU.mult),
                    [bGc, chs], [g])
                bC = nb()
                for k in range(3):
                    PE(lambda k=k, bC=bC, c2=c2: nc.tensor.matmul(bC.t[:, :], diagC.t[:, c2 * 3 + k, :], g.t[:, c2, k:k + TS],
                                                                  start=(k == 0), stop=(k == 2)), [diagC, g], [bC])
                ACT(lambda bGb=bGb: nc.scalar.copy(gb.t[:], bGb.t[:, :]), [bGb], [gb])
                DVE(lambda bC=bC, c2=c2, yabc=yabc: nc.vector.tensor_tensor(yabc.t[:, 4 + c2, :], bC.t[:, :], gb.t[:], ALU.mult),
                    [bC, gb], [yabc])
                POOL(lambda c2=c2: nc.gpsimd.tensor_copy(g.t[:, c2, 0:2], g.t[:, c2, TS:TS + 2]), [g], [g])
            bQ = proj_chunk(12)
            ACT(lambda bQ=bQ, qst=qst: nc.scalar.mul(qst.t[:], bQ.t[:, :], 0.125), [bQ], [qst])
            bK = proj_chunk(13)
            DVE(lambda bK=bK, kst=kst: nc.vector.tensor_copy(kst.t[:], bK.t[:, :]), [bK], [kst])
            for j in range(4):
                bV = nb()
                for k in range(8):
                    PE(lambda k=k, j=j, bV=bV: nc.tensor.matmul(bV.t[:, 0:128], xT.t[:, k, j * 128:(j + 1) * 128],
                                                                w16.t[:, k, 1792:1920], start=(k == 0), stop=(k == 7)),
                       [xT, w16], [bV])
                if j % 2 == 0:
                    ACT(lambda j=j, bV=bV, vst=vst: nc.scalar.copy(vst.t[:, j, :], bV.t[:, 0:128]), [bV], [vst])
                else:
                    DVE(lambda j=j, bV=bV, vst=vst: nc.vector.tensor_copy(vst.t[:, j, :], bV.t[:, 0:128]), [bV], [vst])
            bM = nb()
            for c2 in range(2):
                PE(lambda c2=c2, bM=bM: nc.tensor.matmul(bM.t[:, :], cview(cx, C_ONESDIV, 128), h.t[:, c2, :],
                                                         start=(c2 == 0), stop=(c2 == 1)), [cst, h], [bM])
            bS = nb()
            for c2 in range(2):
                PE(lambda c2=c2, bS=bS: nc.tensor.matmul(bS.t[:, :], cview(cx, C_ONESDIV, 128), hsq.t[:, c2, :],
                                                         start=(c2 == 0), stop=(c2 == 1)), [cst, hsq], [bS])
            ACT(lambda bM=bM: nc.scalar.activation(msq.t[:], bM.t[:, :], AF.Square), [bM], [msq])
            DVE(lambda bS=bS: nc.vector.tensor_tensor(var.t[:], bS.t[:, :], msq.t[:], ALU.subtract), [bS, msq], [var])
            ACT(lambda: nc.scalar.activation(var.t[:], var.t[:], AF.Sqrt, bias=epsb.t[:, 0:1]), [var, epsb], [var])
            DVE(lambda: nc.vector.reciprocal(rstd.t[:], var.t[:]), [var], [rstd])
            for c2 in range(2):
                DVE(lambda c2=c2, bM=bM: nc.vector.tensor_tensor(t1.t[:], h.t[:, c2, :], bM.t[:, :], ALU.subtract),
                    [h, bM], [t1])
                DVE(lambda: nc.vector.tensor_tensor(t2.t[:], t1.t[:], rstd.t[:], ALU.mult), [t1, rstd], [t2])
                ACT(lambda c2=c2, yabc=yabc: nc.scalar.activation(yabc.t[:, c2, :], t2.t[:], AF.Silu,
                                                       scale=pcs.t[:, 64 + c2:65 + c2], bias=pcs.t[:, 66 + c2:67 + c2]),
                    [t2, pcs], [yabc])
            fw.dma(fw.sync, cx.Ysc[:, :, t0:t0 + TS], yabc.t[:], reads=[yabc], writes=[])
            fw.dma(fw.sync, cx.QTsc[:, t0:t0 + TS], qst.t[:], reads=[qst], writes=[])
            fw.dma(fw.sync, cx.KTsc[:, t0:t0 + TS], kst.t[:], reads=[kst], writes=[])
            fw.dma(fw.sync, cx.Vsc[t0:t0 + TS, :].rearrange("(j p) d -> p j d", p=128), vst.t[:], reads=[vst], writes=[])
        fw.barrier()
        fw.phase_end(_pidx)


def layer_norm_tile(cx, ops, xr, lng, lnb, xo, sbufs):
    fw, nc = cx.fw, cx.nc
    PE, ACT, DVE, POOL, nb = ops
    stats, mv, rs, nmr, epsb = sbufs
    for hh in range(2):
        DVE(lambda hh=hh: nc.vector.bn_stats(stats.t[:, hh * 6:(hh + 1) * 6], xr.t[:, hh * 512:(hh + 1) * 512]), [xr], [stats])
    yield
    DVE(lambda: nc.vector.bn_aggr(mv.t[:], stats.t[:]), [stats], [mv])
    yield
    ACT(lambda: nc.scalar.activation(rs.t[:], mv.t[:, 1:2], AF.Sqrt, bias=epsb.t[:, 0:1]), [mv, epsb], [rs])
    yield
    DVE(lambda: nc.vector.reciprocal(rs.t[:], rs.t[:]), [rs], [rs])
    yield
    DVE(lambda: nc.vector.scalar_tensor_tensor(nmr.t[:], mv.t[:, 0:1], -1.0, rs.t[:], ALU.mult, ALU.mult), [mv, rs], [nmr])
    yield
    ACT(lambda: nc.scalar.activation(xr.t[:], xr.t[:], AF.Identity, scale=rs.t[:, 0:1], bias=nmr.t[:, 0:1]),
        [xr, rs, nmr], [xr])
    yield
    POOL(lambda: nc.gpsimd.tensor_tensor(xr.t[:], xr.t[:], lng, ALU.mult), [xr, cx.lnp_r], [xr])
    yield
    POOL(lambda: nc.gpsimd.tensor_tensor(xo.t[:], xr.t[:], lnb, ALU.add), [xr, cx.lnp_r], [xo])
    yield


def lockstep(gens):
    alive = [True] * len(gens)
    while any(alive):
        for gi in range(len(gens)):
            if alive[gi]:
                try:
                    next(gens[gi])
                except StopIteration:
                    alive[gi] = False


def phase_m2a(cx, l):
    from contextlib import ExitStack
    _pidx = cx.fw.phase_begin()
    fw, nc, S = cx.fw, cx.nc, cx.S
    ops = _ops(cx)
    PE, ACT, DVE, POOL, nb = ops
    ob = [cx.banks[6]]
    NST = S // TS
    cst = cx.cst
    negtri = cview(cx, C_NEGTRI, 128)
    negones = cview(cx, C_NEGONES, 128)
    with ExitStack() as st_:
        sb = lambda n, s, d: fw.sbx(st_, n, s, d)
        KT = sb("KT", [128, S], BF16)
        Vc = sb("Vc", [128, S // 128, 128], BF16)
        QTb = [sb("QT%d" % i, [128, TS], BF16) for i in range(2)]
        yd = [sb("yd%d" % i, [128, TS], BF16) for i in range(2)]
        ebuf = [sb("ebuf%d" % i, [128, TS], F32) for i in range(2)]
        Pbuf = [sb("Pbuf%d" % i, [128, TS], F32) for i in range(2)]
        wbuf = [sb("wbuf%d" % i, [128, TS], BF16) for i in range(2)]
        oneb = sb("oneb", [128, 1], F32)
        POOL(lambda: nc.gpsimd.memset(oneb.t[:], 1.0), [], [oneb])
        zA = cx.banks[0:2]
        zB = cx.banks[2:4]
        Psum = [sb("Psum3_%d" % i, [128, TS], F32) for i in range(3)]
        Psb = [sb("Psb%d" % i, [128, TS], BF16) for i in range(3)]
        nob = sb("nob", [128, 128], BF16)
        POOL(lambda: nc.gpsimd.memset(nob.t[:], -1.0), [], [nob])
        gk0 = 0
        KT_r = [fw.res("KT%d" % i) for i in range(NST)]
        V_r = [fw.res("V%d" % i) for i in range(NST)]
        it = 0
        for st in range(NST):
            t0 = st * TS
            QT = QTb[st % 2]
            fw.dma(fw.sync, QT.t[:], cx.QTsc[:, t0:t0 + TS], reads=[], writes=[QT])
            fw.dma(fw.sync, KT.t[:, t0:t0 + TS], cx.KTsc[:, t0:t0 + TS], reads=[], writes=[KT_r[st]], semres=KT)
            fw.dma(fw.sync, Vc.t[:, 4 * st:4 * st + 4, :], cx.Vsc[t0:t0 + TS, :].rearrange("(j p) d -> p j d", p=128),
                   reads=[], writes=[V_r[st]], semres=Vc)
            ydt = yd[st % 2]
            cx.cast_tables(l, (st + 1) * (st + 2) / float(NST * (NST + 1)), [yd[(st - 1) % 2]] if st >= 1 else [])
            iters = []
            for hd in range(2):
                nkb = 4 * st + 4
                for idx in range(nkb):
                    iters.append((hd, idx, nkb - 1 - idx))
            n_it = len(iters)

            def prm(i):
                hd, idx, kb = iters[i]
                return hd, idx, kb, hd // 2, (hd % 2) * 64, gk0 + i

            def S12(i):
                hd, idx, kb, hc, po, k = prm(i)
                zb = zA[k % 2]
                e_, P_ = ebuf[k % 2], Pbuf[k % 2]
                kres = KT_r[kb // 4]
                PE(lambda QT=QT: nc.tensor.matmul(zb.t[:, :], KT.t[po:po + 64, kb * 128:(kb + 1) * 128], QT.t[po:po + 64, :],
                                                  start=True, stop=True), [kres, QT], [zb])
                ACT(lambda: nc.scalar.activation(e_.t[:], zb.t[:, :], AF.Exp), [zb], [e_])
                ACT(lambda: nc.scalar.activation(P_.t[:], e_.t[:], AF.Ln, bias=oneb.t[:, 0:1]), [e_, oneb], [P_])
                if kb >= 4 * st:
                    d = kb - 4 * st
                    DVE(lambda: nc.vector.tensor_tensor(P_.t[:], P_.t[:], cview(cx, C_MASK + d * 512, 512), ALU.mult), [P_, cst], [P_])
                if kb > 0:
                    pn, pb = Psum[k % 3], Psb[k % 3]
                    if idx == 0:
                        DVE(lambda: nc.vector.tensor_copy(pn.t[:], P_.t[:]), [P_], [pn])
                    else:
                        pp = Psum[(k - 1) % 3]
                        DVE(lambda: nc.vector.tensor_tensor(pn.t[:], pp.t[:], P_.t[:], ALU.add), [P_, pp], [pn])
                    DVE(lambda: nc.vector.tensor_copy(pb.t[:], pn.t[:]), [pn], [pb])

            def S34(i):
                hd, idx, kb, hc, po, k = prm(i)
                zb = zB[k % 2]
                P_, w_ = Pbuf[k % 2], wbuf[k % 2]
                kres = KT_r[kb // 4]
                PE(lambda QT=QT: nc.tensor.matmul(zb.t[:, :], KT.t[po:po + 64, kb * 128:(kb + 1) * 128], QT.t[po:po + 64, :],
                                                  start=True, stop=False), [kres, QT], [zb])
                PE(lambda: nc.tensor.matmul(zb.t[:, :], negtri, P_.t[:], start=False, stop=(idx == 0)), [P_, cst], [zb])
                if idx > 0:
                    pb = Psb[(k - 1) % 3]
                    PE(lambda: nc.tensor.matmul(zb.t[:, :], nob.t[:], pb.t[:], start=False, stop=True), [pb, nob], [zb])
                ACT(lambda: nc.scalar.activation(w_.t[:], zb.t[:, :], AF.Exp), [zb], [w_])
                if kb >= 4 * st:
                    d = kb - 4 * st
                    DVE(lambda: nc.vector.tensor_tensor(w_.t[:], w_.t[:], cview(cx, C_MASK + d * 512, 512), ALU.mult), [w_, cst], [w_])

            def S5(i):
                hd, idx, kb, hc, po, k = prm(i)
                o = ob[hc]
                w_ = wbuf[k % 2]
                PE(lambda: nc.tensor.matmul(o.t[po:po + 64, :], Vc.t[:, kb, hd * 64:(hd + 1) * 64], w_.t[:],
                                            start=(idx == 0), stop=(kb == 0)), [V_r[kb // 4], w_], [o])
                if kb == 0:
                    ACT(lambda ydt=ydt: nc.scalar.copy(ydt.t[po:po + 64, :], o.t[po:po + 64, :]), [o], [ydt])

            for step in range(n_it + 2):
                if step < n_it:
                    S12(step)
                if 0 <= step - 1 < n_it:
                    S34(step - 1)
                if 0 <= step - 2 < n_it:
                    S5(step - 2)
            gk0 += n_it
            fw.dma(fw.sync, cx.YDloc[:, t0:t0 + TS], ydt.t[:], reads=[ydt], writes=[])
        fw.barrier()
        fw.phase_end(_pidx)


def phase_m2b(cx, l, xsrc):
    from contextlib import ExitStack
    _pidx = cx.fw.phase_begin()
    fw, nc, NT = cx.fw, cx.nc, cx.NT
    ops = _ops(cx)
    PE, ACT, DVE, POOL, nb = ops
    cx.rot = cx.banks
    cx.bi = 0
    with ExitStack() as st_:
        sb = lambda n, s, d: fw.sbx(st_, n, s, d)
        wo16 = cx.wo16
        yT = [sb("yT%d" % i, [128, 8, TS], BF16) for i in range(2)]
        lnp = sb("lnp", [128, 2 * D], F32)
        xin = [sb("xin%d" % i, [128, D], F32) for i in range(3)]
        xo = [sb("xo%d" % i, [128, D], F32) for i in range(2)]
        epsb = sb("epsb", [128, 1], F32)
        cx.lnp_r = lnp
        fw.dma(fw.sync, lnp.t[:], cx.lnp[l][:, 0:2 * D], reads=[], writes=[lnp])
        POOL(lambda: nc.gpsimd.memset(epsb.t[:], LN_EPS), [], [epsb])
        dyn = cx.dyn
        ydall = cx.YDall.rearrange("(c p) s -> p c s", p=128)
        yown_r, ydown_r = fw.res("yown"), fw.res("ydown")
        fw.dma(fw.sync, cx.Yown, lambda: cx.Ysc[:, :, bass.ds(dyn["off"], NT)], reads=[], writes=[yown_r])
        fw.dma(fw.sync, cx.YDown, lambda: ydall[:, :, bass.ds(dyn["off"], NT)], reads=[], writes=[ydown_r])
        lnset = []
        for i in range(2):
            lnset.append((sb("stats", [128, 12], F32), sb("mv", [128, 2], F32), sb("rs", [128, 1], F32), sb("nmr", [128, 1], F32), epsb))
        xin = xin + [sb("xin3", [128, D], F32)]

        def tile_gen(yt, t0, j, xi, xo_, lns):
            fw.dma(fw.sync, xi.t[:], xsrc[t0 + j * 128:t0 + (j + 1) * 128, :], reads=[], writes=[xi])
            for hh in range(2):
                b = nb()
                for c in range(8):
                    PE(lambda b=b, c=c, hh=hh: nc.tensor.matmul(
                        b.t[:, :], yt.t[:, c, j * 128:(j + 1) * 128], wo16.t[:, c, hh * 512:(hh + 1) * 512],
                        start=(c == 0), stop=(c == 7)), [yt, wo16], [b])
                DVE(lambda b=b, hh=hh: nc.vector.scalar_tensor_tensor(
                    xi.t[:, hh * 512:(hh + 1) * 512], xi.t[:, hh * 512:(hh + 1) * 512], ALPHA, b.t[:, :],
                    ALU.mult, ALU.add), [xi, b], [xi])
            yield
            yield from layer_norm_tile(cx, ops, xi, lnp.t[:, 0:D], lnp.t[:, D:2 * D], xo_, lns)
            fw.dma(fw.sync, cx.X1[t0 + j * 128:t0 + (j + 1) * 128, :], xo_.t[:], reads=[xo_], writes=[])
            yield

        cnt = 0
        for st in range(NT // TS):
            t0 = st * TS
            yt = yT[st % 2]
            fw.dma(fw.sync, yt.t[:, 0:6, :], cx.Yown[:, :, t0:t0 + TS], reads=[yown_r], writes=[yt])
            fw.dma(fw.sync, yt.t[:, 6:8, :], cx.YDown[:, :, t0:t0 + TS], reads=[ydown_r], writes=[yt])
            for jp in range(2):
                gens = []
                for q_ in range(2):
                    j = 2 * jp + q_
                    gens.append(tile_gen(yt, t0, j, xin[cnt % 4], xo[cnt % 2], lnset[q_]))
                    cnt += 1
                lockstep(gens)
        fw.barrier()
        fw.phase_end(_pidx)


def all_gather(cx, src, dst, barrier=True):
    fw, nc = cx.fw, cx.nc
    if barrier:
        fw.barrier()
    cx.ncc += 1
    k = cx.ncc
    sem = cx.ccsem
    groups = cx.groups
    fw.pool.prog.append(lambda: nc.gpsimd.collective_compute("AllGather", ALU.bypass, replica_groups=groups,
                                                             ins=[src.opt()], outs=[dst.opt()]).then_inc(sem, 1))
    for q in fw.queues:
        fw._wait(q, (None, sem, k))


def build_program(S, L, dbg=None, ncores=8):
    nc = bass.Bass("TRN2", target_bir_lowering=False)
    fw = FW(nc)
    cx = CX()
    NT = S // 2
    cx.nc, cx.fw, cx.S, cx.L, cx.NT = nc, fw, S, L, NT
    cx.groups = [[2 * i, 2 * i + 1] for i in range(ncores // 2)]
    cx.ncc = 0
    cx.ccsem = fw.new_sem("cc")
    cx.dyn = {}
    ik = "ExternalInput"

    def din(name, shape, dt=F32):
        return nc.dram_tensor(name, list(shape), dt, kind=ik).ap()

    def dsc(name, shape, dt):
        kind = "ExternalOutput" if (dbg and name in dbg) else "Internal"
        return nc.dram_tensor(name, list(shape), dt, kind=kind).ap()

    def dcc(name, shape, dt):
        return nc.dram_tensor(name, list(shape), dt).ap()

    cx.x_in = din("x", [S, D])
    cx.consts = din("consts", [128, NCONST])
    cx.w_in = din("w_in", [L, D, DIN])
    cx.w_out = din("w_out", [L, D, D])
    cx.pc = din("pc", [L, 128, NPC])
    cx.poolw = din("poolw", [L, 128, 2, 128])
    cx.lnp = din("lnp", [L, 128, 4 * D])
    cx.wq = din("wq", [L, D, 2048])
    cx.keysT = din("keysT", [L, 128, 2048])
    cx.uT = din("uT", [L, 128, 128, 1024])
    cx.vt = din("vt", [L, 128, 128, 1024])
    cx.out = nc.dram_tensor("out", [NT, D], F32, kind="ExternalOutput").ap()
    cx.Ysc = dsc("Ysc", [128, 6, S], BF16)
    cx.QTsc = dsc("QTsc", [128, S], BF16)
    cx.KTsc = dsc("KTsc", [128, S], BF16)
    cx.Vsc = dsc("Vsc", [S, 128], BF16)
    cx.YDloc = dcc("YDloc", [128, S], BF16)
    cx.YDall = dcc("YDall", [256, S], BF16)
    cx.Yown = dsc("Yown", [128, 6, NT], BF16)
    cx.YDown = dsc("YDown", [128, 2, NT], BF16)
    cx.x_own = din("x_own", [NT, D])
    cx.X1 = dsc("X1", [NT, D], F32)
    cx.X2own = dcc("X2own", [NT, D], F32)
    cx.X2g = dcc("X2g", [NT // TS, 2 * TS, D], F32)
    cx.X1T = dsc("X1T", [128, 8, NT], BF16)
    cx.Rsc = dsc("Rsc", [128, 3, NT], F32)
    cx.UT16 = [dsc("UT16_%d" % l, [128, 128, 1024], BF16) for l in range(L)]
    cx.V16 = [dsc("V16_%d" % l, [128, 128, 1024], BF16) for l in range(L)]

    cx.cst = fw.sb("cst", [128, NCONST], F32)
    cx.banks = [fw.ps("bank%d" % i, [128, 512], F32) for i in range(8)]
    fw.dma(fw.sync, cx.cst.t[:], cx.consts, reads=[], writes=[cx.cst])
    fw.sync.prog.append(lambda: cx.dyn.__setitem__("off", (nc.sync.partition_id() % 2) * NT))

    cx.castres = [fw.res("cast%d" % i) for i in range(4)]
    for r_ in cx.castres:
        r_.persist = True
    cx.cast_i = 0

    def cast_tables(l, frac, reads):
        jobs = [(src, dst, c0) for (src, dst) in ((cx.uT[l], cx.UT16[l]), (cx.vt[l], cx.V16[l])) for c0 in range(0, 128, 8)]
        hi = int(round(len(jobs) * frac))
        lo = cx.cast_done.get(l, 0)
        for (src, dst, c0) in jobs[lo:hi]:
            fw.dma(fw.pool, dst[c0:c0 + 8], src[c0:c0 + 8], reads=reads, writes=[], semres=cx.castres[cx.cast_i % 4])
            cx.cast_i += 1
        cx.cast_done[l] = max(lo, hi)
    cx.cast_done = {}
    cx.cast_tables = cast_tables

    for l in range(L):
        nblk = NT // TS
        if l == 0:
            xsrc = lambda st, j: cx.x_in[st * TS + j * 128:st * TS + (j + 1) * 128, :]
        else:
            fw.barrier()
            for k in range(nblk):
                all_gather(cx, cx.X2own[k * TS:(k + 1) * TS, :], cx.X2g[k], barrier=False)
            xsrc = lambda st, j: cx.X2g[st % nblk, (st // nblk) * TS + j * 128:(st // nblk) * TS + (j + 1) * 128, :]
        xdst = cx.out if l == L - 1 else cx.X2own
        phase_m1(cx, l, xsrc)
        from contextlib import ExitStack
        with ExitStack() as ws:
            cx.wo16 = fw.sbx(ws, "wo16", [128, 8, D], BF16)
            cx.wq16 = fw.sbx(ws, "wq16", [128, 8, 2048], BF16)
            cx.kT16 = fw.sbx(ws, "kT16", [128, 16, 128], BF16)
            for r_ in (cx.wo16, cx.wq16, cx.kT16):
                r_.persist = True
            load_cast(cx, cx.wo16, lambda k, c0, c1: cx.wo16.t[:, k, c0:c1], cx.w_out[l], 8, D, 1024)
            load_cast(cx, cx.wq16, lambda k, c0, c1: cx.wq16.t[:, k, c0:c1], cx.wq[l], 8, 2048, 1024)
            fw.dma(fw.pool, cx.kT16.t[:].rearrange("p g n -> p (g n)"), cx.keysT[l], reads=[], writes=[cx.kT16], max_dma_last_dim=4096)
            phase_m2a(cx, l)
            all_gather(cx, cx.YDloc, cx.YDall)
            phase_m2b(cx, l, cx.x_own if l == 0 else cx.X2own)
            phase_p1(cx, l)
            fw.barrier()
            for r_ in (cx.wo16, cx.wq16, cx.kT16):
                if r_.dsem is not None:
                    fw.sem_pool_sw.append((r_.dsem, r_.dcount))
                    r_.dsem = None
                    fw.dres.remove(r_)
        phase_p2(cx, l, xdst)
    fw.barrier()
    fw.emit()
    return nc, cx


def prep_shared(inp, L):
    f = lambda a: np.ascontiguousarray(np.asarray(a, dtype=np.float32))
    d = {}
    d["consts"] = make_consts()
    wi = np.asarray(inp["w_in"][:L], dtype=np.float32)
    d["w_in_r"] = []
    for r in range(2):
        cols = np.concatenate([np.arange(1536), 1536 + r * 128 + np.arange(128), 1792 + r * 128 + np.arange(128),
                               2048 + r * 128 + np.arange(128)])
        d["w_in_r"].append(np.ascontiguousarray(wi[:, :, cols]))
    d["w_out"] = f(inp["w_out"][:L])
    pc = np.zeros((L, 128, NPC), np.float32)
    for l in range(L):
        caw = np.asarray(inp["conv_a_w"][l])
        for c2 in range(2):
            pc[l, :, c2 * 31:(c2 + 1) * 31] = caw[:, c2 * 128:(c2 + 1) * 128].T
            pc[l, :, 62 + c2] = np.asarray(inp["conv_a_b"][l])[c2 * 128:(c2 + 1) * 128]
            pc[l, :, 64 + c2] = np.asarray(inp["norm_a_g"][l])[c2 * 128:(c2 + 1) * 128]
            pc[l, :, 66 + c2] = np.asarray(inp["norm_a_b"][l])[c2 * 128:(c2 + 1) * 128]
            pc[l, :, 68 + c2] = np.asarray(inp["pool_scale"][l])[c2 * 128:(c2 + 1) * 128]
            pc[l, :, 70 + c2 * 3:73 + c2 * 3] = np.asarray(inp["conv_c_w"][l])[:, c2 * 128:(c2 + 1) * 128].T
    d["pc"] = pc
    pw = np.zeros((L, 128, 2, 128), np.float32)
    for l in range(L):
        w = np.asarray(inp["pool_w"][l])
        for gi in range(4):
            c2, o = gi // 2, (gi % 2) * 64
            pw[l, o:o + 64, c2, o:o + 64] = w[gi]
    d["poolw"] = pw
    lnp = np.zeros((L, 128, 4 * D), np.float32)
    for l in range(L):
        row = np.concatenate([np.asarray(inp["ln1_g"][l]), np.asarray(inp["ln1_b"][l]),
                              np.asarray(inp["ln2_g"][l]), np.asarray(inp["ln2_b"][l])])
        lnp[l] = np.broadcast_to(row[None, :], (128, 4 * D))
    d["lnp"] = lnp
    d["wq"] = f(inp["peer_wq"][:L])
    keys = np.asarray(inp["peer_keys"][:L], dtype=np.float32)
    d["keysT"] = np.ascontiguousarray(keys.reshape(L, 16, 128, 128).transpose(0, 3, 1, 2).reshape(L, 128, 2048))
    u = np.asarray(inp["peer_u"][:L], dtype=np.float32)
    d["uT"] = np.ascontiguousarray(u.reshape(L, 128, 128, 8, 128).transpose(0, 1, 4, 3, 2).reshape(L, 128, 128, 1024))
    d["vt"] = np.ascontiguousarray(np.asarray(inp["peer_v"][:L], dtype=np.float32).reshape(L, 128, 128, 1024))
    return d


def topk16(cx, ops, src_fn, src2_fn, val_fn, idx_fn, ngrp, rs_src, rs_src2, rs_val, rs_idx):
    nc = cx.nc
    PE, ACT, DVE, POOL, nb = ops
    for g in range(ngrp):
        DVE(lambda g=g: nc.vector.max(val_fn(g, 0), src_fn(g)), [rs_src[g]], [rs_val[g]])
    yield
    for g in range(ngrp):
        DVE(lambda g=g: nc.vector.max_index(idx_fn(g, 0), val_fn(g, 0), src_fn(g)), [rs_src[g], rs_val[g]], [rs_idx[g]])
    yield
    for g in range(ngrp):
        DVE(lambda g=g: nc.vector.match_replace(src2_fn(g), val_fn(g, 0), src_fn(g), NEG), [rs_src[g], rs_val[g]], [rs_src2[g]])
    yield
    for g in range(ngrp):
        DVE(lambda g=g: nc.vector.max(val_fn(g, 1), src2_fn(g)), [rs_src2[g]], [rs_val[g]])
    yield
    for g in range(ngrp):
        DVE(lambda g=g: nc.vector.max_index(idx_fn(g, 1), val_fn(g, 1), src2_fn(g)), [rs_src2[g], rs_val[g]], [rs_idx[g]])
    yield


def phase_p1(cx, l):
    from contextlib import ExitStack
    _pidx = cx.fw.phase_begin()
    fw, nc, S = cx.fw, cx.nc, cx.NT
    ops = _ops(cx)
    PE, ACT, DVE, POOL, nb = ops
    cx.rot = cx.banks
    cx.bi = 0
    NST = S // TS
    cst = cx.cst
    ident = cview(cx, C_IDENT, 128)
    iota16 = cview(cx, C_IOTA16, 16)
    with ExitStack() as st_:
        sb = lambda n, s, d: fw.sbx(st_, n, s, d)
        wq16, kT16 = cx.wq16, cx.kT16
        xin = [sb("xin%d" % i, [128, D], F32) for i in range(2)]
        x1T = sb("x1T", [128, 8, TS], BF16)
        qT = sb("qT", [128, 16, TS], BF16)
        RT = [sb("RT%d" % i, [128, 3, TS], F32) for i in range(2)]

        class BS:
            pass

        sets = []
        for i in range(2):
            B = BS()
            B.sc = sb("sc", [128, 16, 128], F32)
            B.sc2 = sb("sc2", [128, 16, 128], F32)
            B.stop = sb("stop", [128, 16, 16], F32)
            B.itop = sb("itop", [128, 16, 16], U32)
            B.itopf = sb("itopf", [128, 16, 16], F32)
            B.cand = sb("cand", [128, 8, 256], F32)
            B.cand2 = sb("cand2", [128, 8, 256], F32)
            B.best = sb("best", [128, 8, 16], F32)
            B.bpos = sb("bpos", [128, 8, 16], U32)
            B.a_u = sb("a_u", [128, 128], U32)
            B.b_u = sb("b_u", [128, 128], U32)
            B.af = sb("af", [128, 128], F32)
            B.bf = sb("bf", [128, 128], F32)
            B.oh = [sb("oh%d" % k, [128, 128, 16], BF16) for k in range(2)]
            B.R3 = sb("R3", [128, 3, 128], F32)
            B.gex = sb("gex", [128, 128], F32)
            B.negm = sb("negm", [128, 8], F32)
            B.Z = sb("Z", [128, 8], F32)
            B.rZ = sb("rZ", [128, 8], F32)
            B.sc_r = [fw.res("sc%d" % g) for g in range(16)]
            B.sc2_r = [fw.res("sc2_%d" % g) for g in range(16)]
            B.stop_r = [fw.res("stop%d" % g) for g in range(16)]
            B.itop_r = [fw.res("itop%d" % g) for g in range(16)]
            B.cand_r = [fw.res("cand%d" % g) for g in range(8)]
            B.cand2_r = [fw.res("cand2_%d" % g) for g in range(8)]
            B.best_r = [fw.res("best%d" % g) for g in range(8)]
            B.bpos_r = [fw.res("bpos%d" % g) for g in range(8)]
            B.R3_r = [fw.res("R3_%d" % g) for g in range(3)]
            sets.append(B)

        def tile_body(B, j, RTt):
            sc, sc2, stop, itop, itopf, cand, cand2, best, bpos = B.sc, B.sc2, B.stop, B.itop, B.itopf, B.cand, B.cand2, B.best, B.bpos
            for bi4 in range(4):
                b = nb()
                for i in range(4):
                    hp = bi4 * 4 + i
                    PE(lambda b=b, i=i, hp=hp: nc.tensor.matmul(b.t[:, i * 128:(i + 1) * 128],
                                                                qT.t[:, hp, j * 128:(j + 1) * 128], kT16.t[:, hp, :],
                                                                start=True, stop=True), [qT, kT16], [b])
                ACT(lambda b=b, bi4=bi4: nc.scalar.copy(sc.t[:, bi4 * 4:(bi4 + 1) * 4, :],
                                                        b.t[:, :].rearrange("p (g n) -> p g n", g=4)),
                    [b], B.sc_r[bi4 * 4:(bi4 + 1) * 4])
            yield
            yield from topk16(cx, ops, lambda g: sc.t[:, g, :], lambda g: sc2.t[:, g, :],
                              lambda g, r: stop.t[:, g, r * 8:(r + 1) * 8], lambda g, r: itop.t[:, g, r * 8:(r + 1) * 8],
                              16, B.sc_r, B.sc2_r, B.stop_r, B.itop_r)
            POOL(lambda: nc.gpsimd.tensor_tensor(
                cand.t[:].rearrange("p h (a b) -> p h a b", a=16),
                stop.t[:, 0:16:2, :].unsqueeze(3).to_broadcast([128, 8, 16, 16]),
                stop.t[:, 1:16:2, :].unsqueeze(2).to_broadcast([128, 8, 16, 16]), ALU.add), B.stop_r, B.cand_r)
            POOL(lambda: nc.gpsimd.tensor_copy(itopf.t[:], itop.t[:]), B.itop_r, [itopf])
            yield
            yield from topk16(cx, ops, lambda g: cand.t[:, g, :], lambda g: cand2.t[:, g, :],
                              lambda g, r: best.t[:, g, r * 8:(r + 1) * 8], lambda g, r: bpos.t[:, g, r * 8:(r + 1) * 8],
                              8, B.cand_r, B.cand2_r, B.best_r, B.bpos_r)
            DVE(lambda: nc.vector.tensor_scalar(B.negm.t[:], best.t[:, :, 0], -1.0, None, ALU.mult), B.best_r, [B.negm])
            DVE(lambda: nc.vector.tensor_single_scalar(B.a_u.t[:], bpos.t[:].rearrange("p h k -> p (h k)"), 4,
                                                       ALU.logical_shift_right), B.bpos_r, [B.a_u])
            DVE(lambda: nc.vector.tensor_single_scalar(B.b_u.t[:], bpos.t[:].rearrange("p h k -> p (h k)"), 15,
                                                       ALU.bitwise_and), B.bpos_r, [B.b_u])
            yield
            for hh in range(8):
                ACT(lambda hh=hh: nc.scalar.activation(B.gex.t[:, hh * 16:(hh + 1) * 16], best.t[:, hh, :], AF.Exp,
                                                       bias=B.negm.t[:, hh:hh + 1], accum_out=B.Z.t[:, hh:hh + 1]),
                    [B.best_r[hh], B.negm], [B.gex, B.Z])
            DVE(lambda: nc.vector.tensor_copy(B.af.t[:], B.a_u.t[:]), [B.a_u], [B.af])
            DVE(lambda: nc.vector.tensor_copy(B.bf.t[:], B.b_u.t[:]), [B.b_u], [B.bf])
            yield
            for (xf, off, ri) in ((B.af, 0, 0), (B.bf, 1, 1)):
                oh = B.oh[ri]
                DVE(lambda xf=xf, oh=oh: nc.vector.tensor_tensor(
                    oh.t[:], iota16.unsqueeze(1).to_broadcast([128, 128, 16]),
                    xf.t[:].unsqueeze(2).to_broadcast([128, 128, 16]), ALU.is_equal), [xf, cst], [oh])
                yield
            for (off, ri) in ((0, 0), (1, 1)):
                oh = B.oh[ri]
                POOL(lambda off=off, oh=oh: nc.gpsimd.tensor_tensor(
                    oh.t[:].rearrange("p (h k) a -> p h k a", h=8), oh.t[:].rearrange("p (h k) a -> p h k a", h=8),
                    itopf.t[:, off:16:2, :].unsqueeze(2).to_broadcast([128, 8, 16, 16]), ALU.mult), [oh, itopf], [oh])
                yield
            DVE(lambda: nc.vector.reciprocal(B.rZ.t[:], B.Z.t[:]), [B.Z], [B.rZ])
            for ri in range(2):
                oh = B.oh[ri]
                DVE(lambda ri=ri, oh=oh: nc.vector.reduce_sum(B.R3.t[:, ri, :], oh.t[:], mybir.AxisListType.X), [oh], [B.R3_r[ri]])
                yield
            DVE(lambda: nc.vector.tensor_tensor(B.R3.t[:, 2, :].rearrange("p (h k) -> p h k", h=8),
                                                B.gex.t[:].rearrange("p (h k) -> p h k", h=8),
                                                B.rZ.t[:].unsqueeze(2).to_broadcast([128, 8, 16]), ALU.mult),
                [B.gex, B.rZ], [B.R3_r[2]])
            yield
            b = nb()
            for i in range(3):
                PE(lambda b=b, i=i: nc.tensor.transpose(b.t[:, i * 128:(i + 1) * 128], B.R3.t[:, i, :], ident), [B.R3_r[i], cst], [b])
            ACT(lambda b=b: nc.scalar.copy(RTt.t[:, :, j * 128:(j + 1) * 128],
                                           b.t[:, 0:384].rearrange("p (i n) -> p i n", i=3)), [b], [RTt])
            yield

        for st in range(NST):
            t0 = st * TS
            RTt = RT[st % 2]
            for j in range(4):
                xi = xin[j % 2]
                fw.dma(fw.sync, xi.t[:], cx.X1[t0 + j * 128:t0 + (j + 1) * 128, :], reads=[], writes=[xi])
                for half in range(2):
                    b = nb()
                    for kk in range(4):
                        k = half * 4 + kk
                        PE(lambda b=b, kk=kk, k=k, xi=xi: nc.tensor.transpose(b.t[:, kk * 128:(kk + 1) * 128],
                                                                               xi.t[:, k * 128:(k + 1) * 128], ident),
                           [xi, cst], [b])
                    dst = x1T.t[:, half * 4:(half + 1) * 4, j * 128:(j + 1) * 128]
                    src = b.t[:, :].rearrange("p (k n) -> p k n", k=4)
                    ACT(lambda dst=dst, src=src: nc.scalar.copy(dst, src), [b], [x1T])
            fw.dma(fw.sync, cx.X1T[:, :, t0:t0 + TS], x1T.t[:], reads=[x1T], writes=[])
            for hp in range(16):
                b = nb()
                for k in range(8):
                    PE(lambda b=b, k=k, hp=hp: nc.tensor.matmul(b.t[:, :], wq16.t[:, k, hp * 128:(hp + 1) * 128], x1T.t[:, k, :],
                                                               start=(k == 0), stop=(k == 7)), [wq16, x1T], [b])
                ACT(lambda b=b, hp=hp: nc.scalar.copy(qT.t[:, hp, :], b.t[:, :]), [b], [qT])
            for jp in range(2):
                lockstep([tile_body(sets[0], 2 * jp, RTt), tile_body(sets[1], 2 * jp + 1, RTt)])
            fw.dma(fw.sync, cx.Rsc[:, :, t0:t0 + TS], RTt.t[:], reads=[RTt], writes=[])
        fw.barrier()
        fw.phase_end(_pidx)


TP = 256
GC = 4
SQK = float(np.sqrt(0.044715))
GELU_S = 1.5957691216057308


def phase_p2(cx, l, xdst):
    from contextlib import ExitStack
    _pidx = cx.fw.phase_begin()
    fw, nc, S = cx.fw, cx.nc, cx.NT
    ops = _ops(cx)
    PE, ACT, DVE, POOL, nb = ops
    cx.rot = cx.banks[0:4]
    cx.bi = 0
    ob = cx.banks[4:8]
    NSP = S // TP
    NG = 128 // GC
    cst = cx.cst
    iota = cview(cx, C_IOTA, 128)
    with ExitStack() as st_:
        sb = lambda n, s, d: fw.sbx(st_, n, s, d)
        G = sb("G", [128, TP, 128], BF16)
        Ab = [sb("A%d" % i, [128, 32, 128], BF16) for i in range(2)]
        Bb = [sb("B%d" % i, [128, 32, 128], BF16) for i in range(2)]
        RTb = [sb("RTt%d" % i, [128, 3, TP], F32) for i in range(2)]
        x1T = sb("x1T", [128, 8, TP], BF16)
        x1 = [sb("x1_%d" % i, [128, D], F32) for i in range(2)]
        UTb = [sb("UTb%d" % i, [128, GC, 1024], BF16) for i in range(3)]
        Vb = [sb("Vb%d" % i, [128, GC, 1024], BF16) for i in range(3)]
        sq = [sb("sq%d" % i, [128, TP], F32) for i in range(2)]
        uu = [sb("uu%d" % i, [128, TP], F32) for i in range(2)]
        sg = [sb("sg%d" % i, [128, TP], F32) for i in range(2)]
        xg = [sb("xg%d" % i, [128, TP], F32) for i in range(2)]
        lnp = sb("lnp", [128, 2 * D], F32)
        xo = [sb("xo%d" % i, [128, D], F32) for i in range(2)]
        epsb = sb("epsb", [128, 1], F32)
        lnset = [(sb("stats", [128, 12], F32), sb("mv", [128, 2], F32), sb("rs", [128, 1], F32), sb("nmr", [128, 1], F32), epsb)
                 for i in range(2)]
        cx.lnp_r = lnp
        fw.dma(fw.sync, lnp.t[:], cx.lnp[l][:, 2 * D:4 * D], reads=[], writes=[lnp])
        POOL(lambda: nc.gpsimd.memset(epsb.t[:], LN_EPS), [], [epsb])
        UT16, V16 = cx.UT16[l], cx.V16[l]
        oT = sb("oT", [128, 8, TP], F32)
        oT_r = [fw.res("oT%d" % i) for i in range(4)]
        ident = cview(cx, C_IDENT, 128)
        iob = sb("iob", [128, 128], BF16)
        DVE(lambda: nc.vector.tensor_copy(iob.t[:], iota), [cst], [iob])

        def load_group(cg):
            slot = cg % 3
            fw.dma(fw.sync, UTb[slot].t[:], UT16[cg * GC:(cg + 1) * GC].rearrange("c p f -> p c f"), reads=[], writes=[UTb[slot]])
            fw.dma(fw.sync, Vb[slot].t[:], V16[cg * GC:(cg + 1) * GC].rearrange("c e d -> e c d"), reads=[], writes=[Vb[slot]])

        DEPTH = 3
        a2 = [sb("a2p_%d" % i, [128, TP], BF16) for i in range(DEPTH + 1)]
        A_r = [[fw.res("A%d_%d" % (i, t)) for t in range(32)] for i in range(2)]
        B_r = [[fw.res("B%d_%d" % (i, t)) for t in range(32)] for i in range(2)]
        G_r = [fw.res("G%d" % i) for i in range(TP // 4)]
        NCH = 128
        def prologue(sp):
            t0 = sp * TP
            RTt = RTb[sp % 2]
            fw.dma(fw.sync, RTt.t[:], cx.Rsc[:, :, t0:t0 + TP], reads=[], writes=[RTt])
            fw.dma(fw.sync, x1T.t[:], cx.X1T[:, :, t0:t0 + TP], reads=[], writes=[x1T])
            load_group(0)
            load_group(1)
            for sbk in range(TP // 32):
                A_, B_ = Ab[sbk % 2], Bb[sbk % 2]
                Ar, Br = A_r[sbk % 2], B_r[sbk % 2]
                for tt in range(32):
                    t = sbk * 32 + tt
                    DVE(lambda A_=A_, tt=tt, t=t, RTt=RTt: nc.vector.tensor_scalar(A_.t[:, tt, :], iob.t[:], RTt.t[:, 0, t:t + 1], RTt.t[:, 2, t:t + 1],
                                                                          ALU.is_equal, ALU.mult), [iob, RTt], [Ar[tt]])
                    DVE(lambda B_=B_, tt=tt, t=t, RTt=RTt: nc.vector.tensor_scalar(B_.t[:, tt, :], iob.t[:], RTt.t[:, 1, t:t + 1], None,
                                                                          ALU.is_equal), [iob, RTt], [Br[tt]])
                for q4 in range(8):
                    gb_ = nb()
                    for i in range(4):
                        tt = q4 * 4 + i
                        PE(lambda gb_=gb_, i=i, tt=tt, A_=A_, B_=B_: nc.tensor.matmul(gb_.t[:, i * 128:(i + 1) * 128], B_.t[:, tt, :], A_.t[:, tt, :],
                                                                                      start=True, stop=True), [Ar[tt], Br[tt]], [gb_])
                    tg = sbk * 32 + q4 * 4
                    ACT(lambda gb_=gb_, tg=tg: nc.scalar.copy(G.t[:, tg:tg + 4, :], gb_.t[:, :].rearrange("p (t n) -> p t n", t=4)),
                        [gb_], [G_r[tg // 4]])

        prologue(0)
        for sp in range(NSP):
            t0 = sp * TP

            def stage_h(c):
                slot = (c // GC) % 3
                cc = c % GC
                i2 = c % 2
                ia = c % (DEPTH + 1)
                hb = nb()
                for dk in range(8):
                    PE(lambda hb=hb, dk=dk, cc=cc, slot=slot: nc.tensor.matmul(hb.t[:, 0:TP], UTb[slot].t[:, cc, dk * 128:(dk + 1) * 128], x1T.t[:, dk, :],
                                                                               start=(dk == 0), stop=(dk == 7)), [UTb[slot], x1T], [hb])
                ACT(lambda hb=hb, i2=i2: nc.scalar.activation(sq[i2].t[:], hb.t[:, 0:TP], AF.Square, scale=SQK), [hb], [sq[i2]])
                DVE(lambda hb=hb, i2=i2: nc.vector.scalar_tensor_tensor(uu[i2].t[:], sq[i2].t[:], 1.0, hb.t[:, 0:TP], ALU.add, ALU.mult),
                    [sq[i2], hb], [uu[i2]])
                ACT(lambda i2=i2: nc.scalar.activation(sg[i2].t[:], uu[i2].t[:], AF.Sigmoid, scale=GELU_S), [uu[i2]], [sg[i2]])
                DVE(lambda hb=hb, i2=i2, c=c: nc.vector.tensor_tensor(xg[i2].t[:], hb.t[:, 0:TP], G.t[:, :, c], ALU.mult), [hb] + G_r, [xg[i2]])
                POOL(lambda i2=i2, ia=ia: nc.gpsimd.tensor_tensor(a2[ia].t[:], sg[i2].t[:], xg[i2].t[:], ALU.mult), [sg[i2], xg[i2]], [a2[ia]])

            def stage_o(c):
                slot = (c // GC) % 3
                cc = c % GC
                ia = c % (DEPTH + 1)
                for dk in range(8):
                    o = ob[dk // 2]
                    PE(lambda o=o, dk=dk, ia=ia, cc=cc, slot=slot, c=c: nc.tensor.matmul(
                        o.t[:, (dk % 2) * TP:(dk % 2 + 1) * TP], Vb[slot].t[:, cc, dk * 128:(dk + 1) * 128], a2[ia].t[:, :],
                        start=(c == 0 and dk % 2 == 0), stop=(c == NCH - 1), skip_group_check=True), [a2[ia], Vb[slot]], [o])

            for c in range(DEPTH):
                stage_h(c)
            for c in range(NCH):
                if c % GC == 0 and c // GC + 2 < NG:
                    load_group(c // GC + 2)
                if c + DEPTH < NCH:
                    stage_h(c + DEPTH)
                stage_o(c)
            if sp + 1 < NSP:
                prologue(sp + 1)

            for k in range(4):
                if k % 2 == 0:
                    ACT(lambda k=k: nc.scalar.copy(oT.t[:, 2 * k:2 * k + 2, :], ob[k].t[:, :].rearrange("p (c t) -> p c t", c=2)), [ob[k]], [oT_r[k]])
                else:
                    DVE(lambda k=k: nc.vector.tensor_copy(oT.t[:, 2 * k:2 * k + 2, :], ob[k].t[:, :].rearrange("p (c t) -> p c t", c=2)), [ob[k]], [oT_r[k]])
            tb = {}
            for j in range(2):
                for hh in range(2):
                    b_ = nb()
                    tb[(j, hh)] = b_
                    for q_ in range(4):
                        dk = hh * 4 + q_
                        PE(lambda b_=b_, q_=q_, dk=dk, j=j: nc.tensor.transpose(b_.t[:, q_ * 128:(q_ + 1) * 128], oT.t[:, dk, j * 128:(j + 1) * 128], ident),
                           [oT_r[dk // 2], cst], [b_])

            def ln2_gen(j):
                xi, xo_ = x1[j], xo[j]
                fw.dma(fw.sync, xi.t[:], cx.X1[t0 + j * 128:t0 + (j + 1) * 128, :], reads=[], writes=[xi])
                for hh in range(2):
                    o = tb[(j, hh)]
                    DVE(lambda o=o, hh=hh: nc.vector.scalar_tensor_tensor(
                        xi.t[:, hh * 512:(hh + 1) * 512], xi.t[:, hh * 512:(hh + 1) * 512], ALPHA, o.t[:, :],
                        ALU.mult, ALU.add), [xi, o], [xi])
                yield
                yield from layer_norm_tile(cx, ops, xi, lnp.t[:, 0:D], lnp.t[:, D:2 * D], xo_, lnset[j])
                fw.dma(fw.sync, xdst[t0 + j * 128:t0 + (j + 1) * 128, :], xo_.t[:], reads=[xo_], writes=[],
                       final=(xdst is cx.out))
                yield
            lockstep([ln2_gen(0), ln2_gen(1)])
        fw.barrier()
        fw.phase_end(_pidx)


SEQ = 8192
NLAYER = 2
_CACHE = {}


def kernel(**inputs):
    x = np.asarray(inputs["x"], dtype=np.float32)
    shared = prep_shared(inputs, NLAYER)
    w_in_r = shared.pop("w_in_r")
    if "nc" not in _CACHE:
        _CACHE["nc"] = build_program(SEQ, NLAYER)[0]
    nc = _CACHE["nc"]
    in_maps = []
    for c in range(8):
        m = dict(shared)
        m["x"] = np.ascontiguousarray(x[c // 2])
        m["w_in"] = w_in_r[c % 2]
        m["x_own"] = np.ascontiguousarray(x[c // 2, (c % 2) * (SEQ // 2):(c % 2 + 1) * (SEQ // 2)])
        in_maps.append(m)
    res = run_bass_kernel_spmd(nc, in_maps, core_ids=list(range(8)))
    outs = [np.asarray(res.results[c]["out"], dtype=np.float32) for c in range(8)]
    return np.stack([np.concatenate([outs[2 * b], outs[2 * b + 1]], axis=0) for b in range(4)])
```

```python
import numpy as np
import concourse.bass as bass
import concourse.mybir as mybir
from concourse.bass_utils import run_bass_kernel_spmd

F32 = mybir.dt.float32
BF16 = mybir.dt.bfloat16
U32 = mybir.dt.uint32
I32 = mybir.dt.int32
AF = mybir.ActivationFunctionType
ALU = mybir.AluOpType


class Res:
    def __init__(self, name, t=None):
        self.name = name
        self.t = t
        self.w = None
        self.r = {}
        self.dsem = None
        self.dcount = 0


class Queue:
    def __init__(self, fw, eng, name, is_pe=False):
        self.eng = eng
        self.name = name
        self.sem = fw.new_sem("q_" + name)
        self.n = 0
        self.seen = {}
        self.prog = []
        self.is_pe = is_pe
        self.hist = []
        self.absorbed = {}


class FW:
    def __init__(self, nc):
        self.nc = nc
        self.nsem = 0
        self.pe = Queue(self, nc.tensor, "pe", True)
        self.act = Queue(self, nc.scalar, "act")
        self.dve = Queue(self, nc.vector, "dve")
        self.pool = Queue(self, nc.gpsimd, "pool")
        self.sync = Queue(self, nc.sync, "sync")
        self.queues = [self.pe, self.act, self.dve, self.pool, self.sync]
        self.finals = []
        self.ninstr = 0
        self.dres = []
        self.sem_pool = []
        self.sem_pool_sw = []

    def new_sem(self, name):
        self.nsem += 1
        return self.nc.alloc_semaphore("s%d_%s" % (self.nsem, name))

    def sb(self, name, shape, dt):
        return Res(name, self.nc.alloc_sbuf_tensor("g_" + name, list(shape), dt))

    def ps(self, name, shape, dt=F32):
        return Res(name, self.nc.alloc_psum_tensor(name, list(shape), dt))

    def res(self, name):
        return Res(name)

    def _wait(self, q, ev):
        q2, sem, val = ev
        k = id(sem)
        if q.seen.get(k, 0) >= val:
            return
        q.seen[k] = val
        q.hist.append((q.n, k, val))
        q.prog.append(lambda: q.eng.wait_ge(sem, val))
        if isinstance(q2, Queue) and q2 is not q:
            pos = q.absorbed.get(id(q2), 0)
            h = q2.hist
            while pos < len(h) and h[pos][0] < val:
                _, k2, v2 = h[pos]
                if q.seen.get(k2, 0) < v2:
                    q.seen[k2] = v2
                    q.hist.append((q.n, k2, v2))
                pos += 1
            q.absorbed[id(q2)] = pos

    def _deps(self, q, reads, writes, is_dma):
        for r in reads:
            if r.w is not None:
                if is_dma or not (r.w[0] is q and q.is_pe):
                    self._wait(q, r.w)
        for w in writes:
            if w.w is not None and (is_dma or not (w.w[0] is q and q.is_pe)):
                self._wait(q, w.w)
            for ev in w.r.values():
                if is_dma or not (ev[0] is q and q.is_pe):
                    self._wait(q, ev)

    def op(self, q, fn, reads, writes):
        self._deps(q, reads, writes, False)
        q.n += 1
        n = q.n
        sem = q.sem
        q.prog.append(lambda: fn().then_inc(sem, 1))
        ev = (q, sem, n)
        for r in reads:
            r.r[id(q)] = ev
        for w in writes:
            w.w = ev
            w.r = {}
        self.ninstr += 1

    def dma(self, q, out_ap, in_ap, reads, writes, semres=None, final=False, **kw):
        if semres is None:
            semres = (writes + reads)[0]
        if semres.dsem is None:
            pool = self.sem_pool_sw if q is self.pool else self.sem_pool
            if pool:
                semres.dsem, semres.dcount = pool.pop()
            else:
                semres.dsem = self.new_sem("d_" + semres.name)
            semres.dsw = (q is self.pool)
            self.dres.append(semres)
        assert semres.dsw == (q is self.pool), semres.name
        self._deps(q, reads, writes, True)
        if semres.dcount:
            self._wait(q, (None, semres.dsem, semres.dcount))
        semres.dcount += 16
        sem, val = semres.dsem, semres.dcount
        eng = q.eng
        cv = lambda a: a() if callable(a) else a
        q.prog.append(lambda: eng.dma_start(out=cv(out_ap), in_=cv(in_ap), **kw).then_inc(sem, 16))
        ev = (None, sem, val)
        for r in reads:
            r.r[("d", id(sem))] = ev
        for w in writes:
            w.w = ev
            w.r = {}
        if final:
            self.finals.append(ev)
        self.ninstr += 1

    def emit(self):
        for ev in self.finals:
            self._wait(self.sync, ev)
        nc = self.nc
        with nc.Block() as block:
            @block.tensor
            def _(e):
                for f in self.pe.prog:
                    f()

            @block.scalar
            def _(e):
                for f in self.act.prog:
                    f()

            @block.vector
            def _(e):
                for f in self.dve.prog:
                    f()

            @block.gpsimd
            def _(e):
                for f in self.pool.prog:
                    f()

            @block.sync
            def _(e):
                for f in self.sync.prog:
                    f()

    def barrier(self):
        sems = []
        for q in self.queues:
            sems.append((q, q.sem, q.n))
        for ev in self.dma_events():
            sems.append(ev)
        for q in self.queues:
            for ev in sems:
                if ev[0] is q or ev[2] == 0:
                    continue
                self._wait(q, ev)

    def phase_begin(self):
        return len(self.dres)

    def phase_end(self, idx):
        keep = []
        for r in self.dres[idx:]:
            if getattr(r, "persist", False):
                keep.append(r)
            else:
                (self.sem_pool_sw if r.dsw else self.sem_pool).append((r.dsem, r.dcount))
                r.dsem = None
        self.dres = self.dres[:idx] + keep

    def dma_events(self):
        return [(None, r.dsem, r.dcount) for r in self.dres if r.dcount]

    def sbx(self, stack, name, shape, dt):
        self.nsb = getattr(self, "nsb", 0) + 1
        t = stack.enter_context(self.nc.sbuf_tensor("%s_%d" % (name, self.nsb), list(shape), dt))
        return Res(name, t)


D = 1024
DIN = 1920
NPC = 76
ALPHA = float((2 * 2) ** 0.25)
LN_EPS = 1e-5
TS = 512
NEG = -1.0e30
C_IDENT = 0
C_NEGTRI = 128
C_NEGONES = 256
C_ONESDIV = 384
C_MASK = 512
C_INVC0 = C_MASK + 2048
C_INVW = C_INVC0 + 32
C_IOTA = C_INVW + 2
C_IOTA16 = C_IOTA + 128
NCONST = C_IOTA16 + 16


def make_consts():
    c = np.zeros((128, NCONST), np.float32)
    c[:, C_IDENT:C_IDENT + 128] = np.eye(128, dtype=np.float32)
    j = np.arange(128)[:, None]
    s = np.arange(128)[None, :]
    c[:, C_NEGTRI:C_NEGTRI + 128] = -(j >= s).astype(np.float32)
    c[:, C_NEGONES:C_NEGONES + 128] = -1.0
    c[:, C_ONESDIV:C_ONESDIV + 128] = 1.0 / 256.0
    t = np.arange(512)[None, :]
    for d in range(4):
        c[:, C_MASK + d * 512:C_MASK + (d + 1) * 512] = ((128 * d + j) < t).astype(np.float32)
    wins = (2, 4, 8, 16)
    for c2 in range(2):
        for p in range(128):
            win = wins[2 * c2 + p // 64]
            c[p, C_INVC0 + c2 * 16:C_INVC0 + (c2 + 1) * 16] = 1.0 / np.minimum(np.arange(16) + 1, win)
            c[p, C_INVW + c2] = 1.0 / win
    c[:, C_IOTA:C_IOTA + 128] = np.arange(128, dtype=np.float32)[None, :]
    c[:, C_IOTA16:C_IOTA16 + 16] = np.arange(16, dtype=np.float32)[None, :]
    return c


class CX:
    pass


def _ops(cx):
    fw, nc = cx.fw, cx.nc

    def PE(fn, r, w):
        fw.op(fw.pe, fn, r, w)

    def ACT(fn, r, w):
        fw.op(fw.act, fn, r, w)

    def DVE(fn, r, w):
        fw.op(fw.dve, fn, r, w)

    def POOL(fn, r, w):
        fw.op(fw.pool, fn, r, w)

    def nb():
        cx.bi = (cx.bi + 1) % len(cx.rot)
        return cx.rot[cx.bi]

    return PE, ACT, DVE, POOL, nb


def cview(cx, off, n):
    return cx.cst.t[:, off:off + n]


def load_cast(cx, dst, dst_ap_fn, src_ap, nk, ncols, step):
    fw = cx.fw
    v = src_ap.rearrange("(k p) n -> p k n", p=128)
    for k in range(nk):
        for c0 in range(0, ncols, step):
            c1 = min(ncols, c0 + step)
            fw.dma(fw.pool, dst_ap_fn(k, c0, c1), v[:, k, c0:c1], reads=[], writes=[dst])


def phase_m1(cx, l, xsrc):
    from contextlib import ExitStack
    _pidx = cx.fw.phase_begin()
    fw, nc, S = cx.fw, cx.nc, cx.S
    PE, ACT, DVE, POOL, nb = _ops(cx)
    cx.rot = cx.banks
    cx.bi = 0
    NST = S // TS
    cst = cx.cst
    ident = cview(cx, C_IDENT, 128)
    with ExitStack() as st_:
        sb = lambda n, s, d: fw.sbx(st_, n, s, d)
        w16 = sb("w16", [128, 8, DIN], BF16)
        pcs = sb("pcs", [128, NPC], F32)
        diagA = sb("diagA", [128, 62, 128], BF16)
        diagC = sb("diagC", [128, 6, 128], BF16)
        pw16 = sb("pw16", [128, 2, 128], BF16)
        xin = [sb("xin%d" % i, [128, D], F32) for i in range(2)]
        xT = sb("xT", [128, 8, TS], BF16)
        u = sb("u", [128, 2, 30 + TS], BF16)
        p = sb("p", [128, 2, 15 + TS], F32)
        g = sb("g", [128, 2, 2 + TS], BF16)
        sA = sb("sA", [128, 2, 15 + TS], F32)
        sB = sb("sB", [128, 2, 15 + TS], F32)
        sg = sb("sg", [128, TS], F32)
        h = sb("h", [128, 2, TS], F32)
        hsq = sb("hsq", [128, 2, TS], F32)
        msq = sb("msq", [128, TS], F32)
        var = sb("var", [128, TS], F32)
        rstd = sb("rstd", [128, TS], F32)
        t1 = sb("t1", [128, TS], F32)
        t2 = sb("t2", [128, TS], F32)
        tmp16 = sb("tmp16", [128, 16], F32)
        pooled = sb("pooled", [128, 2, TS], BF16)
        yabc_b = [sb("yabc%d" % i, [128, 6, TS], BF16) for i in range(2)]
        qst_b = [sb("qst%d" % i, [128, TS], BF16) for i in range(2)]
        kst_b = [sb("kst%d" % i, [128, TS], BF16) for i in range(2)]
        vst_b = [sb("vst%d" % i, [128, 4, 128], BF16) for i in range(2)]
        chs = sb("chs", [128, TS], F32)
        gb = sb("gb", [128, TS], F32)
        epsb = sb("epsb", [128, 1], F32)

        fw.dma(fw.sync, pcs.t[:], cx.pc[l], reads=[], writes=[pcs])
        load_cast(cx, w16, lambda k, c0, c1: w16.t[:, k, c0:c1], cx.w_in[l], 8, DIN, DIN)
        fw.dma(fw.pool, pw16.t[:], cx.poolw[l], reads=[], writes=[pw16])
        for i in range(62):
            DVE(lambda i=i: nc.vector.tensor_scalar(diagA.t[:, i, :], ident, pcs.t[:, i:i + 1], None, ALU.mult),
                [cst, pcs], [diagA])
        for i in range(6):
            DVE(lambda i=i: nc.vector.tensor_scalar(diagC.t[:, i, :], ident, pcs.t[:, 70 + i:71 + i], None, ALU.mult),
                [cst, pcs], [diagC])
        POOL(lambda: nc.gpsimd.memset(u.t[:, :, 0:30], 0.0), [], [u])
        POOL(lambda: nc.gpsimd.memset(p.t[:, :, 0:15], 0.0), [], [p])
        POOL(lambda: nc.gpsimd.memset(g.t[:, :, 0:2], 0.0), [], [g])
        POOL(lambda: nc.gpsimd.memset(sA.t[:], 0.0), [], [sA])
        POOL(lambda: nc.gpsimd.memset(sB.t[:], 0.0), [], [sB])
        POOL(lambda: nc.gpsimd.memset(epsb.t[:], LN_EPS), [], [epsb])
        def proj_chunk(cidx):
            b = nb()
            for k in range(8):
                PE(lambda k=k, b=b: nc.tensor.matmul(b.t[:, :], w16.t[:, k, cidx * 128:(cidx + 1) * 128], xT.t[:, k, :],
                                                     start=(k == 0), stop=(k == 7)), [w16, xT], [b])
            return b

        cnt = 0
        for st in range(NST):
            t0 = st * TS
            yabc, qst, kst, vst = yabc_b[st % 2], qst_b[st % 2], kst_b[st % 2], vst_b[st % 2]
            for j in range(4):
                xi = xin[j % 2]
                fw.dma(fw.sync, xi.t[:], xsrc(st, j), reads=[], writes=[xi])
                for half in range(2):
                    b = nb()
                    for kk in range(4):
                        k = half * 4 + kk
                        PE(lambda b=b, kk=kk, k=k, xi=xi: nc.tensor.transpose(b.t[:, kk * 128:(kk + 1) * 128],
                                                                               xi.t[:, k * 128:(k + 1) * 128], ident),
                           [xi, cst], [b])
                    dst = xT.t[:, half * 4:(half + 1) * 4, j * 128:(j + 1) * 128]
                    src = b.t[:, :].rearrange("p (k n) -> p k n", k=4)
                    if (cnt % 2) == 0:
                        ACT(lambda dst=dst, src=src: nc.scalar.copy(dst, src), [b], [xT])
                    else:
                        DVE(lambda dst=dst, src=src: nc.vector.tensor_copy(dst, src), [b], [xT])
                    cnt += 1
            for c2 in range(2):
                bA = proj_chunk(0 + c2)
                bG = proj_chunk(2 + c2)
                ACT(lambda bG=bG: nc.scalar.activation(sg.t[:], bG.t[:, :], AF.Sigmoid), [bG], [sg])
                DVE(lambda bA=bA, c2=c2: nc.vector.tensor_tensor(u.t[:, c2, 30:30 + TS], bA.t[:, :], sg.t[:], ALU.mult),
                    [bA, sg], [u])
                bH = nb()
                for k in range(31):
                    PE(lambda k=k, bH=bH, c2=c2: nc.tensor.matmul(bH.t[:, :], diagA.t[:, c2 * 31 + k, :], u.t[:, c2, k:k + TS],
                                                                  start=(k == 0), stop=(k == 30)), [diagA, u], [bH])
                ACT(lambda bH=bH, c2=c2: nc.scalar.activation(h.t[:, c2, :], bH.t[:, :], AF.Identity,
                                                              bias=pcs.t[:, 62 + c2:63 + c2]), [bH, pcs], [h])
                ACT(lambda bH=bH, c2=c2: nc.scalar.activation(hsq.t[:, c2, :], bH.t[:, :], AF.Square,
                                                              bias=pcs.t[:, 62 + c2:63 + c2]), [bH, pcs], [hsq])
                POOL(lambda c2=c2: nc.gpsimd.tensor_copy(u.t[:, c2, 0:30], u.t[:, c2, TS:TS + 30]), [u], [u])
            bM = nb()
            for c2 in range(2):
                PE(lambda c2=c2, bM=bM: nc.tensor.matmul(bM.t[:, :], cview(cx, C_ONESDIV, 128), h.t[:, c2, :],
                                                         start=(c2 == 0), stop=(c2 == 1)), [cst, h], [bM])
            bS = nb()
            for c2 in range(2):
                PE(lambda c2=c2, bS=bS: nc.tensor.matmul(bS.t[:, :], cview(cx, C_ONESDIV, 128), hsq.t[:, c2, :],
                                                         start=(c2 == 0), stop=(c2 == 1)), [cst, hsq], [bS])
            ACT(lambda bM=bM: nc.scalar.activation(msq.t[:], bM.t[:, :], AF.Square), [bM], [msq])
            DVE(lambda bS=bS: nc.vector.tensor_tensor(var.t[:], bS.t[:, :], msq.t[:], ALU.subtract), [bS, msq], [var])
            ACT(lambda: nc.scalar.activation(var.t[:], var.t[:], AF.Sqrt, bias=epsb.t[:, 0:1]), [var, epsb], [var])
            DVE(lambda: nc.vector.reciprocal(rstd.t[:], var.t[:]), [var], [rstd])
            for c2 in range(2):
                DVE(lambda c2=c2, bM=bM: nc.vector.tensor_tensor(t1.t[:], h.t[:, c2, :], bM.t[:, :], ALU.subtract),
                    [h, bM], [t1])
                DVE(lambda: nc.vector.tensor_tensor(t2.t[:], t1.t[:], rstd.t[:], ALU.mult), [t1, rstd], [t2])
                ACT(lambda c2=c2, yabc=yabc: nc.scalar.activation(yabc.t[:, c2, :], t2.t[:], AF.Silu,
                                                       scale=pcs.t[:, 64 + c2:65 + c2], bias=pcs.t[:, 66 + c2:67 + c2]),
                    [t2, pcs], [yabc])
            for c2 in range(2):
                bP = proj_chunk(4 + c2)
                ACT(lambda bP=bP, c2=c2: nc.scalar.copy(p.t[:, c2, 15:15 + TS], bP.t[:, :]), [bP], [p])
                W = 15 + TS
                DVE(lambda c2=c2: nc.vector.tensor_tensor(sA.t[:, c2, 1:W], p.t[:, c2, 1:W], p.t[:, c2, 0:W - 1], ALU.add),
                    [p], [sA])
                DVE(lambda c2=c2: nc.vector.tensor_tensor(sB.t[:, c2, 3:W], sA.t[:, c2, 3:W], sA.t[:, c2, 1:W - 2], ALU.add),
                    [sA], [sB])
                if c2 == 1:
                    DVE(lambda: nc.vector.tensor_tensor(sA.t[:, 1, 7:W], sB.t[:, 1, 7:W], sB.t[:, 1, 3:W - 4], ALU.add),
                        [sB], [sA])
                    DVE(lambda: nc.vector.tensor_tensor(sB.t[:, 1, 15:W], sA.t[:, 1, 15:W], sA.t[:, 1, 7:W - 8], ALU.add),
                        [sA], [sB])
                for (lo, srcb) in ((0, sA), (64, sB)):
                    DVE(lambda lo=lo, srcb=srcb, c2=c2: nc.vector.scalar_tensor_tensor(
                        pooled.t[lo:lo + 64, c2, :], srcb.t[lo:lo + 64, c2, 15:W],
                        cst.t[lo:lo + 64, C_INVW + c2:C_INVW + c2 + 1], p.t[lo:lo + 64, c2, 15:W],
                        ALU.mult, ALU.subtract), [srcb, p, cst], [pooled])
                    if st == 0:
                        DVE(lambda lo=lo, srcb=srcb, c2=c2: nc.vector.tensor_tensor(
                            tmp16.t[lo:lo + 64, :], srcb.t[lo:lo + 64, c2, 15:31],
                            cst.t[lo:lo + 64, C_INVC0 + c2 * 16:C_INVC0 + (c2 + 1) * 16], ALU.mult), [srcb, cst], [tmp16])
                        DVE(lambda lo=lo, c2=c2: nc.vector.tensor_tensor(
                            pooled.t[lo:lo + 64, c2, 0:16], tmp16.t[lo:lo + 64, :], p.t[lo:lo + 64, c2, 15:31],
                            ALU.subtract), [tmp16, p], [pooled])
                bX = nb()
                PE(lambda bX=bX, c2=c2: nc.tensor.matmul(bX.t[:, :], pw16.t[:, c2, :], pooled.t[:, c2, :], start=True, stop=True),
                   [pw16, pooled], [bX])
                ACT(lambda bX=bX, c2=c2, yabc=yabc: nc.scalar.activation(yabc.t[:, 2 + c2, :], bX.t[:, :], AF.Copy,
                                                              scale=pcs.t[:, 68 + c2:69 + c2]), [bX, pcs], [yabc])
                POOL(lambda c2=c2: nc.gpsimd.tensor_copy(p.t[:, c2, 0:15], p.t[:, c2, TS:TS + 15]), [p], [p])
            for c2 in range(2):
                bCh = proj_chunk(6 + c2)
                bGb = proj_chunk(8 + c2)
                bGc = proj_chunk(10 + c2)
                ACT(lambda bCh=bCh: nc.scalar.copy(chs.t[:], bCh.t[:, :]), [bCh], [chs])
                DVE(lambda bGc=bGc, c2=c2: nc.vector.tensor_tensor(g.t[:, c2, 2:2 + TS], bGc.t[:, :], chs.t[:], ALU.mult),
                    [bGc, chs], [g])
                bC = nb()
                for k in range(3):
                    PE(lambda k=k, bC=bC, c2=c2: nc.tensor.matmul(bC.t[:, :], diagC.t[:, c2 * 3 + k, :], g.t[:, c2, k:k + TS],
                                                                  start=(k == 0), stop=(k == 2)), [diagC, g], [bC])
                ACT(lambda bGb=bGb: nc.scalar.copy(gb.t[:], bGb.t[:, :]), [bGb], [gb])
                DVE(lambda bC=bC, c2=c2, yabc=yabc: nc.vector.tensor_tensor(yabc.t[:, 4 + c2, :], bC.t[:, :], gb.t[:], ALU.mult),
                    [bC, gb], [yabc])
                POOL(lambda c2=c2: nc.gpsimd.tensor_copy(g.t[:, c2, 0:2], g.t[:, c2, TS:TS + 2]), [g], [g])
            bQ = proj_chunk(12)
            ACT(lambda bQ=bQ, qst=qst: nc.scalar.mul(qst.t[:], bQ.t[:, :], 0.125), [bQ], [qst])
            bK = proj_chunk(13)
            DVE(lambda bK=bK, kst=kst: nc.vector.tensor_copy(kst.t[:], bK.t[:, :]), [bK], [kst])
            for j in range(4):
                bV = nb()
                for k in range(8):
                    PE(lambda k=k, j=j, bV=bV: nc.tensor.matmul(bV.t[:, 0:128], xT.t[:, k, j * 128:(j + 1) * 128],
                                                                w16.t[:, k, 1792:1920], start=(k == 0), stop=(k == 7)),
                       [xT, w16], [bV])
                if j % 2 == 0:
                    ACT(lambda j=j, bV=bV, vst=vst: nc.scalar.copy(vst.t[:, j, :], bV.t[:, 0:128]), [bV], [vst])
                else:
                    DVE(lambda j=j, bV=bV, vst=vst: nc.vector.tensor_copy(vst.t[:, j, :], bV.t[:, 0:128]), [bV], [vst])
            fw.dma(fw.sync, cx.Ysc[:, :, t0:t0 + TS], yabc.t[:], reads=[yabc], writes=[])
            fw.dma(fw.sync, cx.QTsc[:, t0:t0 + TS], qst.t[:], reads=[qst], writes=[])
            fw.dma(fw.sync, cx.KTsc[:, t0:t0 + TS], kst.t[:], reads=[kst], writes=[])
            fw.dma(fw.sync, cx.Vsc[t0:t0 + TS, :].rearrange("(j p) d -> p j d", p=128), vst.t[:], reads=[vst], writes=[])
        fw.barrier()
        fw.phase_end(_pidx)


def layer_norm_tile(cx, ops, xr, lng, lnb, xo, sbufs):
    fw, nc = cx.fw, cx.nc
    PE, ACT, DVE, POOL, nb = ops
    stats, mv, rs, nmr, epsb = sbufs
    for hh in range(2):
        DVE(lambda hh=hh: nc.vector.bn_stats(stats.t[:, hh * 6:(hh + 1) * 6], xr.t[:, hh * 512:(hh + 1) * 512]), [xr], [stats])
    yield
    DVE(lambda: nc.vector.bn_aggr(mv.t[:], stats.t[:]), [stats], [mv])
    yield
    ACT(lambda: nc.scalar.activation(rs.t[:], mv.t[:, 1:2], AF.Sqrt, bias=epsb.t[:, 0:1]), [mv, epsb], [rs])
    yield
    DVE(lambda: nc.vector.reciprocal(rs.t[:], rs.t[:]), [rs], [rs])
    yield
    DVE(lambda: nc.vector.scalar_tensor_tensor(nmr.t[:], mv.t[:, 0:1], -1.0, rs.t[:], ALU.mult, ALU.mult), [mv, rs], [nmr])
    yield
    ACT(lambda: nc.scalar.activation(xr.t[:], xr.t[:], AF.Identity, scale=rs.t[:, 0:1], bias=nmr.t[:, 0:1]),
        [xr, rs, nmr], [xr])
    yield
    POOL(lambda: nc.gpsimd.tensor_tensor(xr.t[:], xr.t[:], lng, ALU.mult), [xr, cx.lnp_r], [xr])
    yield
    POOL(lambda: nc.gpsimd.tensor_tensor(xo.t[:], xr.t[:], lnb, ALU.add), [xr, cx.lnp_r], [xo])
    yield


def lockstep(gens):
    alive = [True] * len(gens)
    while any(alive):
        for gi in range(len(gens)):
            if alive[gi]:
                try:
                    next(gens[gi])
                except StopIteration:
                    alive[gi] = False


def phase_m2a(cx, l):
    from contextlib import ExitStack
    _pidx = cx.fw.phase_begin()
    fw, nc, S = cx.fw, cx.nc, cx.S
    ops = _ops(cx)
    PE, ACT, DVE, POOL, nb = ops
    ob = [cx.banks[6]]
    NST = S // TS
    cst = cx.cst
    negtri = cview(cx, C_NEGTRI, 128)
    negones = cview(cx, C_NEGONES, 128)
    with ExitStack() as st_:
        sb = lambda n, s, d: fw.sbx(st_, n, s, d)
        KT = sb("KT", [128, S], BF16)
        Vc = sb("Vc", [128, S // 128, 128], BF16)
        QTb = [sb("QT%d" % i, [128, TS], BF16) for i in range(2)]
        yd = [sb("yd%d" % i, [128, TS], BF16) for i in range(2)]
        ebuf = [sb("ebuf%d" % i, [128, TS], F32) for i in range(2)]
        Pbuf = [sb("Pbuf%d" % i, [128, TS], F32) for i in range(2)]
        wbuf = [sb("wbuf%d" % i, [128, TS], BF16) for i in range(2)]
        oneb = sb("oneb", [128, 1], F32)
        POOL(lambda: nc.gpsimd.memset(oneb.t[:], 1.0), [], [oneb])
        zA = cx.banks[0:2]
        zB = cx.banks[2:4]
        Psum = [sb("Psum3_%d" % i, [128, TS], F32) for i in range(3)]
        Psb = [sb("Psb%d" % i, [128, TS], BF16) for i in range(3)]
        nob = sb("nob", [128, 128], BF16)
        POOL(lambda: nc.gpsimd.memset(nob.t[:], -1.0), [], [nob])
        gk0 = 0
        KT_r = [fw.res("KT%d" % i) for i in range(NST)]
        V_r = [fw.res("V%d" % i) for i in range(NST)]
        it = 0
        for st in range(NST):
            t0 = st * TS
            QT = QTb[st % 2]
            fw.dma(fw.sync, QT.t[:], cx.QTsc[:, t0:t0 + TS], reads=[], writes=[QT])
            fw.dma(fw.sync, KT.t[:, t0:t0 + TS], cx.KTsc[:, t0:t0 + TS], reads=[], writes=[KT_r[st]], semres=KT)
            fw.dma(fw.sync, Vc.t[:, 4 * st:4 * st + 4, :], cx.Vsc[t0:t0 + TS, :].rearrange("(j p) d -> p j d", p=128),
                   reads=[], writes=[V_r[st]], semres=Vc)
            ydt = yd[st % 2]
            cx.cast_tables(l, (st + 1) * (st + 2) / float(NST * (NST + 1)), [yd[(st - 1) % 2]] if st >= 1 else [])
            iters = []
            for hd in range(2):
                nkb = 4 * st + 4
                for idx in range(nkb):
                    iters.append((hd, idx, nkb - 1 - idx))
            n_it = len(iters)

            def prm(i):
                hd, idx, kb = iters[i]
                return hd, idx, kb, hd // 2, (hd % 2) * 64, gk0 + i

            def S12(i):
                hd, idx, kb, hc, po, k = prm(i)
                zb = zA[k % 2]
                e_, P_ = ebuf[k % 2], Pbuf[k % 2]
                kres = KT_r[kb // 4]
                PE(lambda QT=QT: nc.tensor.matmul(zb.t[:, :], KT.t[po:po + 64, kb * 128:(kb + 1) * 128], QT.t[po:po + 64, :],
                                                  start=True, stop=True), [kres, QT], [zb])
                ACT(lambda: nc.scalar.activation(e_.t[:], zb.t[:, :], AF.Exp), [zb], [e_])
                ACT(lambda: nc.scalar.activation(P_.t[:], e_.t[:], AF.Ln, bias=oneb.t[:, 0:1]), [e_, oneb], [P_])
                if kb >= 4 * st:
                    d = kb - 4 * st
                    DVE(lambda: nc.vector.tensor_tensor(P_.t[:], P_.t[:], cview(cx, C_MASK + d * 512, 512), ALU.mult), [P_, cst], [P_])
                if kb > 0:
                    pn, pb = Psum[k % 3], Psb[k % 3]
                    if idx == 0:
                        DVE(lambda: nc.vector.tensor_copy(pn.t[:], P_.t[:]), [P_], [pn])
                    else:
                        pp = Psum[(k - 1) % 3]
                        DVE(lambda: nc.vector.tensor_tensor(pn.t[:], pp.t[:], P_.t[:], ALU.add), [P_, pp], [pn])
                    DVE(lambda: nc.vector.tensor_copy(pb.t[:], pn.t[:]), [pn], [pb])

            def S34(i):
                hd, idx, kb, hc, po, k = prm(i)
                zb = zB[k % 2]
                P_, w_ = Pbuf[k % 2], wbuf[k % 2]
                kres = KT_r[kb // 4]
                PE(lambda QT=QT: nc.tensor.matmul(zb.t[:, :], KT.t[po:po + 64, kb * 128:(kb + 1) * 128], QT.t[po:po + 64, :],
                                                  start=True, stop=False), [kres, QT], [zb])
                PE(lambda: nc.tensor.matmul(zb.t[:, :], negtri, P_.t[:], start=False, stop=(idx == 0)), [P_, cst], [zb])
                if idx > 0:
                    pb = Psb[(k - 1) % 3]
                    PE(lambda: nc.tensor.matmul(zb.t[:, :], nob.t[:], pb.t[:], start=False, stop=True), [pb, nob], [zb])
                ACT(lambda: nc.scalar.activation(w_.t[:], zb.t[:, :], AF.Exp), [zb], [w_])
                if kb >= 4 * st:
                    d = kb - 4 * st
                    DVE(lambda: nc.vector.tensor_tensor(w_.t[:], w_.t[:], cview(cx, C_MASK + d * 512, 512), ALU.mult), [w_, cst], [w_])

            def S5(i):
                hd, idx, kb, hc, po, k = prm(i)
                o = ob[hc]
                w_ = wbuf[k % 2]
                PE(lambda: nc.tensor.matmul(o.t[po:po + 64, :], Vc.t[:, kb, hd * 64:(hd + 1) * 64], w_.t[:],
                                            start=(idx == 0), stop=(kb == 0)), [V_r[kb // 4], w_], [o])
                if kb == 0:
                    ACT(lambda ydt=ydt: nc.scalar.copy(ydt.t[po:po + 64, :], o.t[po:po + 64, :]), [o], [ydt])

            for step in range(n_it + 2):
                if step < n_it:
                    S12(step)
                if 0 <= step - 1 < n_it:
                    S34(step - 1)
                if 0 <= step - 2 < n_it:
                    S5(step - 2)
            gk0 += n_it
            fw.dma(fw.sync, cx.YDloc[:, t0:t0 + TS], ydt.t[:], reads=[ydt], writes=[])
        fw.barrier()
        fw.phase_end(_pidx)


def phase_m2b(cx, l, xsrc):
    from contextlib import ExitStack
    _pidx = cx.fw.phase_begin()
    fw, nc, NT = cx.fw, cx.nc, cx.NT
    ops = _ops(cx)
    PE, ACT, DVE, POOL, nb = ops
    cx.rot = cx.banks
    cx.bi = 0
    with ExitStack() as st_:
        sb = lambda n, s, d: fw.sbx(st_, n, s, d)
        wo16 = cx.wo16
        yT = [sb("yT%d" % i, [128, 8, TS], BF16) for i in range(2)]
        lnp = sb("lnp", [128, 2 * D], F32)
        xin = [sb("xin%d" % i, [128, D], F32) for i in range(3)]
        xo = [sb("xo%d" % i, [128, D], F32) for i in range(2)]
        epsb = sb("epsb", [128, 1], F32)
        cx.lnp_r = lnp
        fw.dma(fw.sync, lnp.t[:], cx.lnp[l][:, 0:2 * D], reads=[], writes=[lnp])
        POOL(lambda: nc.gpsimd.memset(epsb.t[:], LN_EPS), [], [epsb])
        dyn = cx.dyn
        ydall = cx.YDall.rearrange("(c p) s -> p c s", p=128)
        yown_r, ydown_r = fw.res("yown"), fw.res("ydown")
        fw.dma(fw.sync, cx.Yown, lambda: cx.Ysc[:, :, bass.ds(dyn["off"], NT)], reads=[], writes=[yown_r])
        fw.dma(fw.sync, cx.YDown, lambda: ydall[:, :, bass.ds(dyn["off"], NT)], reads=[], writes=[ydown_r])
        lnset = []
        for i in range(2):
            lnset.append((sb("stats", [128, 12], F32), sb("mv", [128, 2], F32), sb("rs", [128, 1], F32), sb("nmr", [128, 1], F32), epsb))
        xin = xin + [sb("xin3", [128, D], F32)]

        def tile_gen(yt, t0, j, xi, xo_, lns):
            fw.dma(fw.sync, xi.t[:], xsrc[t0 + j * 128:t0 + (j + 1) * 128, :], reads=[], writes=[xi])
            for hh in range(2):
                b = nb()
                for c in range(8):
                    PE(lambda b=b, c=c, hh=hh: nc.tensor.matmul(
                        b.t[:, :], yt.t[:, c, j * 128:(j + 1) * 128], wo16.t[:, c, hh * 512:(hh + 1) * 512],
                        start=(c == 0), stop=(c == 7)), [yt, wo16], [b])
                DVE(lambda b=b, hh=hh: nc.vector.scalar_tensor_tensor(
                    xi.t[:, hh * 512:(hh + 1) * 512], xi.t[:, hh * 512:(hh + 1) * 512], ALPHA, b.t[:, :],
                    ALU.mult, ALU.add), [xi, b], [xi])
            yield
            yield from layer_norm_tile(cx, ops, xi, lnp.t[:, 0:D], lnp.t[:, D:2 * D], xo_, lns)
            fw.dma(fw.sync, cx.X1[t0 + j * 128:t0 + (j + 1) * 128, :], xo_.t[:], reads=[xo_], writes=[])
            yield

        cnt = 0
        for st in range(NT // TS):
            t0 = st * TS
            yt = yT[st % 2]
            fw.dma(fw.sync, yt.t[:, 0:6, :], cx.Yown[:, :, t0:t0 + TS], reads=[yown_r], writes=[yt])
            fw.dma(fw.sync, yt.t[:, 6:8, :], cx.YDown[:, :, t0:t0 + TS], reads=[ydown_r], writes=[yt])
            for jp in range(2):
                gens = []
                for q_ in range(2):
                    j = 2 * jp + q_
                    gens.append(tile_gen(yt, t0, j, xin[cnt % 4], xo[cnt % 2], lnset[q_]))
                    cnt += 1
                lockstep(gens)
        fw.barrier()
        fw.phase_end(_pidx)


def all_gather(cx, src, dst, barrier=True):
    fw, nc = cx.fw, cx.nc
    if barrier:
        fw.barrier()
    cx.ncc += 1
    k = cx.ncc
    sem = cx.ccsem
    groups = cx.groups
    fw.pool.prog.append(lambda: nc.gpsimd.collective_compute("AllGather", ALU.bypass, replica_groups=groups,
                                                             ins=[src.opt()], outs=[dst.opt()]).then_inc(sem, 1))
    for q in fw.queues:
        fw._wait(q, (None, sem, k))


def build_program(S, L, dbg=None, ncores=8):
    nc = bass.Bass("TRN2", target_bir_lowering=False)
    fw = FW(nc)
    cx = CX()
    NT = S // 2
    cx.nc, cx.fw, cx.S, cx.L, cx.NT = nc, fw, S, L, NT
    cx.groups = [[2 * i, 2 * i + 1] for i in range(ncores // 2)]
    cx.ncc = 0
    cx.ccsem = fw.new_sem("cc")
    cx.dyn = {}
    ik = "ExternalInput"

    def din(name, shape, dt=F32):
        return nc.dram_tensor(name, list(shape), dt, kind=ik).ap()

    def dsc(name, shape, dt):
        kind = "ExternalOutput" if (dbg and name in dbg) else "Internal"
        return nc.dram_tensor(name, list(shape), dt, kind=kind).ap()

    def dcc(name, shape, dt):
        return nc.dram_tensor(name, list(shape), dt).ap()

    cx.x_in = din("x", [S, D])
    cx.consts = din("consts", [128, NCONST])
    cx.w_in = din("w_in", [L, D, DIN])
    cx.w_out = din("w_out", [L, D, D])
    cx.pc = din("pc", [L, 128, NPC])
    cx.poolw = din("poolw", [L, 128, 2, 128])
    cx.lnp = din("lnp", [L, 128, 4 * D])
    cx.wq = din("wq", [L, D, 2048])
    cx.keysT = din("keysT", [L, 128, 2048])
    cx.uT = din("uT", [L, 128, 128, 1024])
    cx.vt = din("vt", [L, 128, 128, 1024])
    cx.out = nc.dram_tensor("out", [NT, D], F32, kind="ExternalOutput").ap()
    cx.Ysc = dsc("Ysc", [128, 6, S], BF16)
    cx.QTsc = dsc("QTsc", [128, S], BF16)
    cx.KTsc = dsc("KTsc", [128, S], BF16)
    cx.Vsc = dsc("Vsc", [S, 128], BF16)
    cx.YDloc = dcc("YDloc", [128, S], BF16)
    cx.YDall = dcc("YDall", [256, S], BF16)
    cx.Yown = dsc("Yown", [128, 6, NT], BF16)
    cx.YDown = dsc("YDown", [128, 2, NT], BF16)
    cx.x_own = din("x_own", [NT, D])
    cx.X1 = dsc("X1", [NT, D], F32)
    cx.X2own = dcc("X2own", [NT, D], F32)
    cx.X2g = dcc("X2g", [NT // TS, 2 * TS, D], F32)
    cx.X1T = dsc("X1T", [128, 8, NT], BF16)
    cx.Rsc = dsc("Rsc", [128, 3, NT], F32)
    cx.UT16 = [dsc("UT16_%d" % l, [128, 128, 1024], BF16) for l in range(L)]
    cx.V16 = [dsc("V16_%d" % l, [128, 128, 1024], BF16) for l in range(L)]

    cx.cst = fw.sb("cst", [128, NCONST], F32)
    cx.banks = [fw.ps("bank%d" % i, [128, 512], F32) for i in range(8)]
    fw.dma(fw.sync, cx.cst.t[:], cx.consts, reads=[], writes=[cx.cst])
    fw.sync.prog.append(lambda: cx.dyn.__setitem__("off", (nc.sync.partition_id() % 2) * NT))

    cx.castres = [fw.res("cast%d" % i) for i in range(4)]
    for r_ in cx.castres:
        r_.persist = True
    cx.cast_i = 0

    def cast_tables(l, frac, reads):
        jobs = [(src, dst, c0) for (src, dst) in ((cx.uT[l], cx.UT16[l]), (cx.vt[l], cx.V16[l])) for c0 in range(0, 128, 8)]
        hi = int(round(len(jobs) * frac))
        lo = cx.cast_done.get(l, 0)
        for (src, dst, c0) in jobs[lo:hi]:
            fw.dma(fw.pool, dst[c0:c0 + 8], src[c0:c0 + 8], reads=reads, writes=[], semres=cx.castres[cx.cast_i % 4])
            cx.cast_i += 1
        cx.cast_done[l] = max(lo, hi)
    cx.cast_done = {}
    cx.cast_tables = cast_tables

    for l in range(L):
        nblk = NT // TS
        if l == 0:
            xsrc = lambda st, j: cx.x_in[st * TS + j * 128:st * TS + (j + 1) * 128, :]
        else:
            fw.barrier()
            for k in range(nblk):
                all_gather(cx, cx.X2own[k * TS:(k + 1) * TS, :], cx.X2g[k], barrier=False)
            xsrc = lambda st, j: cx.X2g[st % nblk, (st // nblk) * TS + j * 128:(st // nblk) * TS + (j + 1) * 128, :]
        xdst = cx.out if l == L - 1 else cx.X2own
        phase_m1(cx, l, xsrc)
        from contextlib import ExitStack
        with ExitStack() as ws:
            cx.wo16 = fw.sbx(ws, "wo16", [128, 8, D], BF16)
            cx.wq16 = fw.sbx(ws, "wq16", [128, 8, 2048], BF16)
            cx.kT16 = fw.sbx(ws, "kT16", [128, 16, 128], BF16)
            for r_ in (cx.wo16, cx.wq16, cx.kT16):
                r_.persist = True
            load_cast(cx, cx.wo16, lambda k, c0, c1: cx.wo16.t[:, k, c0:c1], cx.w_out[l], 8, D, 1024)
            load_cast(cx, cx.wq16, lambda k, c0, c1: cx.wq16.t[:, k, c0:c1], cx.wq[l], 8, 2048, 1024)
            fw.dma(fw.pool, cx.kT16.t[:].rearrange("p g n -> p (g n)"), cx.keysT[l], reads=[], writes=[cx.kT16], max_dma_last_dim=4096)
            phase_m2a(cx, l)
            all_gather(cx, cx.YDloc, cx.YDall)
            phase_m2b(cx, l, cx.x_own if l == 0 else cx.X2own)
            phase_p1(cx, l)
            fw.barrier()
            for r_ in (cx.wo16, cx.wq16, cx.kT16):
                if r_.dsem is not None:
                    fw.sem_pool_sw.append((r_.dsem, r_.dcount))
                    r_.dsem = None
                    fw.dres.remove(r_)
        phase_p2(cx, l, xdst)
    fw.barrier()
    fw.emit()
    return nc, cx


def prep_shared(inp, L):
    f = lambda a: np.ascontiguousarray(np.asarray(a, dtype=np.float32))
    d = {}
    d["consts"] = make_consts()
    wi = np.asarray(inp["w_in"][:L], dtype=np.float32)
    d["w_in_r"] = []
    for r in range(2):
        cols = np.concatenate([np.arange(1536), 1536 + r * 128 + np.arange(128), 1792 + r * 128 + np.arange(128),
                               2048 + r * 128 + np.arange(128)])
        d["w_in_r"].append(np.ascontiguousarray(wi[:, :, cols]))
    d["w_out"] = f(inp["w_out"][:L])
    pc = np.zeros((L, 128, NPC), np.float32)
    for l in range(L):
        caw = np.asarray(inp["conv_a_w"][l])
        for c2 in range(2):
            pc[l, :, c2 * 31:(c2 + 1) * 31] = caw[:, c2 * 128:(c2 + 1) * 128].T
            pc[l, :, 62 + c2] = np.asarray(inp["conv_a_b"][l])[c2 * 128:(c2 + 1) * 128]
            pc[l, :, 64 + c2] = np.asarray(inp["norm_a_g"][l])[c2 * 128:(c2 + 1) * 128]
            pc[l, :, 66 + c2] = np.asarray(inp["norm_a_b"][l])[c2 * 128:(c2 + 1) * 128]
            pc[l, :, 68 + c2] = np.asarray(inp["pool_scale"][l])[c2 * 128:(c2 + 1) * 128]
            pc[l, :, 70 + c2 * 3:73 + c2 * 3] = np.asarray(inp["conv_c_w"][l])[:, c2 * 128:(c2 + 1) * 128].T
    d["pc"] = pc
    pw = np.zeros((L, 128, 2, 128), np.float32)
    for l in range(L):
        w = np.asarray(inp["pool_w"][l])
        for gi in range(4):
            c2, o = gi // 2, (gi % 2) * 64
            pw[l, o:o + 64, c2, o:o + 64] = w[gi]
    d["poolw"] = pw
    lnp = np.zeros((L, 128, 4 * D), np.float32)
    for l in range(L):
        row = np.concatenate([np.asarray(inp["ln1_g"][l]), np.asarray(inp["ln1_b"][l]),
                              np.asarray(inp["ln2_g"][l]), np.asarray(inp["ln2_b"][l])])
        lnp[l] = np.broadcast_to(row[None, :], (128, 4 * D))
    d["lnp"] = lnp
    d["wq"] = f(inp["peer_wq"][:L])
    keys = np.asarray(inp["peer_keys"][:L], dtype=np.float32)
    d["keysT"] = np.ascontiguousarray(keys.reshape(L, 16, 128, 128).transpose(0, 3, 1, 2).reshape(L, 128, 2048))
    u = np.asarray(inp["peer_u"][:L], dtype=np.float32)
    d["uT"] = np.ascontiguousarray(u.reshape(L, 128, 128, 8, 128).transpose(0, 1, 4, 3, 2).reshape(L, 128, 128, 1024))
    d["vt"] = np.ascontiguousarray(np.asarray(inp["peer_v"][:L], dtype=np.float32).reshape(L, 128, 128, 1024))
    return d


def topk16(cx, ops, src_fn, src2_fn, val_fn, idx_fn, ngrp, rs_src, rs_src2, rs_val, rs_idx):
    nc = cx.nc
    PE, ACT, DVE, POOL, nb = ops
    for g in range(ngrp):
        DVE(lambda g=g: nc.vector.max(val_fn(g, 0), src_fn(g)), [rs_src[g]], [rs_val[g]])
    yield
    for g in range(ngrp):
        DVE(lambda g=g: nc.vector.max_index(idx_fn(g, 0), val_fn(g, 0), src_fn(g)), [rs_src[g], rs_val[g]], [rs_idx[g]])
    yield
    for g in range(ngrp):
        DVE(lambda g=g: nc.vector.match_replace(src2_fn(g), val_fn(g, 0), src_fn(g), NEG), [rs_src[g], rs_val[g]], [rs_src2[g]])
    yield
    for g in range(ngrp):
        DVE(lambda g=g: nc.vector.max(val_fn(g, 1), src2_fn(g)), [rs_src2[g]], [rs_val[g]])
    yield
    for g in range(ngrp):
        DVE(lambda g=g: nc.vector.max_index(idx_fn(g, 1), val_fn(g, 1), src2_fn(g)), [rs_src2[g], rs_val[g]], [rs_idx[g]])
    yield


def phase_p1(cx, l):
    from contextlib import ExitStack
    _pidx = cx.fw.phase_begin()
    fw, nc, S = cx.fw, cx.nc, cx.NT
    ops = _ops(cx)
    PE, ACT, DVE, POOL, nb = ops
    cx.rot = cx.banks
    cx.bi = 0
    NST = S // TS
    cst = cx.cst
    ident = cview(cx, C_IDENT, 128)
    iota16 = cview(cx, C_IOTA16, 16)
    with ExitStack() as st_:
        sb = lambda n, s, d: fw.sbx(st_, n, s, d)
        wq16, kT16 = cx.wq16, cx.kT16
        xin = [sb("xin%d" % i, [128, D], F32) for i in range(2)]
        x1T = sb("x1T", [128, 8, TS], BF16)
        qT = sb("qT", [128, 16, TS], BF16)
        RT = [sb("RT%d" % i, [128, 3, TS], F32) for i in range(2)]

        class BS:
            pass

        sets = []
        for i in range(2):
            B = BS()
            B.sc = sb("sc", [128, 16, 128], F32)
            B.sc2 = sb("sc2", [128, 16, 128], F32)
            B.stop = sb("stop", [128, 16, 16], F32)
            B.itop = sb("itop", [128, 16, 16], U32)
            B.itopf = sb("itopf", [128, 16, 16], F32)
            B.cand = sb("cand", [128, 8, 256], F32)
            B.cand2 = sb("cand2", [128, 8, 256], F32)
            B.best = sb("best", [128, 8, 16], F32)
            B.bpos = sb("bpos", [128, 8, 16], U32)
            B.a_u = sb("a_u", [128, 128], U32)
            B.b_u = sb("b_u", [128, 128], U32)
            B.af = sb("af", [128, 128], F32)
            B.bf = sb("bf", [128, 128], F32)
            B.oh = [sb("oh%d" % k, [128, 128, 16], BF16) for k in range(2)]
            B.R3 = sb("R3", [128, 3, 128], F32)
            B.gex = sb("gex", [128, 128], F32)
            B.negm = sb("negm", [128, 8], F32)
            B.Z = sb("Z", [128, 8], F32)
            B.rZ = sb("rZ", [128, 8], F32)
            B.sc_r = [fw.res("sc%d" % g) for g in range(16)]
            B.sc2_r = [fw.res("sc2_%d" % g) for g in range(16)]
            B.stop_r = [fw.res("stop%d" % g) for g in range(16)]
            B.itop_r = [fw.res("itop%d" % g) for g in range(16)]
            B.cand_r = [fw.res("cand%d" % g) for g in range(8)]
            B.cand2_r = [fw.res("cand2_%d" % g) for g in range(8)]
            B.best_r = [fw.res("best%d" % g) for g in range(8)]
            B.bpos_r = [fw.res("bpos%d" % g) for g in range(8)]
            B.R3_r = [fw.res("R3_%d" % g) for g in range(3)]
            sets.append(B)

        def tile_body(B, j, RTt):
            sc, sc2, stop, itop, itopf, cand, cand2, best, bpos = B.sc, B.sc2, B.stop, B.itop, B.itopf, B.cand, B.cand2, B.best, B.bpos
            for bi4 in range(4):
                b = nb()
                for i in range(4):
                    hp = bi4 * 4 + i
                    PE(lambda b=b, i=i, hp=hp: nc.tensor.matmul(b.t[:, i * 128:(i + 1) * 128],
                                                                qT.t[:, hp, j * 128:(j + 1) * 128], kT16.t[:, hp, :],
                                                                start=True, stop=True), [qT, kT16], [b])
                ACT(lambda b=b, bi4=bi4: nc.scalar.copy(sc.t[:, bi4 * 4:(bi4 + 1) * 4, :],
                                                        b.t[:, :].rearrange("p (g n) -> p g n", g=4)),
                    [b], B.sc_r[bi4 * 4:(bi4 + 1) * 4])
            yield
            yield from topk16(cx, ops, lambda g: sc.t[:, g, :], lambda g: sc2.t[:, g, :],
                              lambda g, r: stop.t[:, g, r * 8:(r + 1) * 8], lambda g, r: itop.t[:, g, r * 8:(r + 1) * 8],
                              16, B.sc_r, B.sc2_r, B.stop_r, B.itop_r)
            POOL(lambda: nc.gpsimd.tensor_tensor(
                cand.t[:].rearrange("p h (a b) -> p h a b", a=16),
                stop.t[:, 0:16:2, :].unsqueeze(3).to_broadcast([128, 8, 16, 16]),
                stop.t[:, 1:16:2, :].unsqueeze(2).to_broadcast([128, 8, 16, 16]), ALU.add), B.stop_r, B.cand_r)
            POOL(lambda: nc.gpsimd.tensor_copy(itopf.t[:], itop.t[:]), B.itop_r, [itopf])
            yield
            yield from topk16(cx, ops, lambda g: cand.t[:, g, :], lambda g: cand2.t[:, g, :],
                              lambda g, r: best.t[:, g, r * 8:(r + 1) * 8], lambda g, r: bpos.t[:, g, r * 8:(r + 1) * 8],
                              8, B.cand_r, B.cand2_r, B.best_r, B.bpos_r)
            DVE(lambda: nc.vector.tensor_scalar(B.negm.t[:], best.t[:, :, 0], -1.0, None, ALU.mult), B.best_r, [B.negm])
            DVE(lambda: nc.vector.tensor_single_scalar(B.a_u.t[:], bpos.t[:].rearrange("p h k -> p (h k)"), 4,
                                                       ALU.logical_shift_right), B.bpos_r, [B.a_u])
            DVE(lambda: nc.vector.tensor_single_scalar(B.b_u.t[:], bpos.t[:].rearrange("p h k -> p (h k)"), 15,
                                                       ALU.bitwise_and), B.bpos_r, [B.b_u])
            yield
            for hh in range(8):
                ACT(lambda hh=hh: nc.scalar.activation(B.gex.t[:, hh * 16:(hh + 1) * 16], best.t[:, hh, :], AF.Exp,
                                                       bias=B.negm.t[:, hh:hh + 1], accum_out=B.Z.t[:, hh:hh + 1]),
                    [B.best_r[hh], B.negm], [B.gex, B.Z])
            DVE(lambda: nc.vector.tensor_copy(B.af.t[:], B.a_u.t[:]), [B.a_u], [B.af])
            DVE(lambda: nc.vector.tensor_copy(B.bf.t[:], B.b_u.t[:]), [B.b_u], [B.bf])
            yield
            for (xf, off, ri) in ((B.af, 0, 0), (B.bf, 1, 1)):
                oh = B.oh[ri]
                DVE(lambda xf=xf, oh=oh: nc.vector.tensor_tensor(
                    oh.t[:], iota16.unsqueeze(1).to_broadcast([128, 128, 16]),
                    xf.t[:].unsqueeze(2).to_broadcast([128, 128, 16]), ALU.is_equal), [xf, cst], [oh])
                yield
            for (off, ri) in ((0, 0), (1, 1)):
                oh = B.oh[ri]
                POOL(lambda off=off, oh=oh: nc.gpsimd.tensor_tensor(
                    oh.t[:].rearrange("p (h k) a -> p h k a", h=8), oh.t[:].rearrange("p (h k) a -> p h k a", h=8),
                    itopf.t[:, off:16:2, :].unsqueeze(2).to_broadcast([128, 8, 16, 16]), ALU.mult), [oh, itopf], [oh])
                yield
            DVE(lambda: nc.vector.reciprocal(B.rZ.t[:], B.Z.t[:]), [B.Z], [B.rZ])
            for ri in range(2):
                oh = B.oh[ri]
                DVE(lambda ri=ri, oh=oh: nc.vector.reduce_sum(B.R3.t[:, ri, :], oh.t[:], mybir.AxisListType.X), [oh], [B.R3_r[ri]])
                yield
            DVE(lambda: nc.vector.tensor_tensor(B.R3.t[:, 2, :].rearrange("p (h k) -> p h k", h=8),
                                                B.gex.t[:].rearrange("p (h k) -> p h k", h=8),
                                                B.rZ.t[:].unsqueeze(2).to_broadcast([128, 8, 16]), ALU.mult),
                [B.gex, B.rZ], [B.R3_r[2]])
            yield
            b = nb()
            for i in range(3):
                PE(lambda b=b, i=i: nc.tensor.transpose(b.t[:, i * 128:(i + 1) * 128], B.R3.t[:, i, :], ident), [B.R3_r[i], cst], [b])
            ACT(lambda b=b: nc.scalar.copy(RTt.t[:, :, j * 128:(j + 1) * 128],
                                           b.t[:, 0:384].rearrange("p (i n) -> p i n", i=3)), [b], [RTt])
            yield

        for st in range(NST):
            t0 = st * TS
            RTt = RT[st % 2]
            for j in range(4):
                xi = xin[j % 2]
                fw.dma(fw.sync, xi.t[:], cx.X1[t0 + j * 128:t0 + (j + 1) * 128, :], reads=[], writes=[xi])
                for half in range(2):
                    b = nb()
                    for kk in range(4):
                        k = half * 4 + kk
                        PE(lambda b=b, kk=kk, k=k, xi=xi: nc.tensor.transpose(b.t[:, kk * 128:(kk + 1) * 128],
                                                                               xi.t[:, k * 128:(k + 1) * 128], ident),
                           [xi, cst], [b])
                    dst = x1T.t[:, half * 4:(half + 1) * 4, j * 128:(j + 1) * 128]
                    src = b.t[:, :].rearrange("p (k n) -> p k n", k=4)
                    ACT(lambda dst=dst, src=src: nc.scalar.copy(dst, src), [b], [x1T])
            fw.dma(fw.sync, cx.X1T[:, :, t0:t0 + TS], x1T.t[:], reads=[x1T], writes=[])
            for hp in range(16):
                b = nb()
                for k in range(8):
                    PE(lambda b=b, k=k, hp=hp: nc.tensor.matmul(b.t[:, :], wq16.t[:, k, hp * 128:(hp + 1) * 128], x1T.t[:, k, :],
                                                               start=(k == 0), stop=(k == 7)), [wq16, x1T], [b])
                ACT(lambda b=b, hp=hp: nc.scalar.copy(qT.t[:, hp, :], b.t[:, :]), [b], [qT])
            for jp in range(2):
                lockstep([tile_body(sets[0], 2 * jp, RTt), tile_body(sets[1], 2 * jp + 1, RTt)])
            fw.dma(fw.sync, cx.Rsc[:, :, t0:t0 + TS], RTt.t[:], reads=[RTt], writes=[])
        fw.barrier()
        fw.phase_end(_pidx)


TP = 256
GC = 4
SQK = float(np.sqrt(0.044715))
GELU_S = 1.5957691216057308


def phase_p2(cx, l, xdst):
    from contextlib import ExitStack
    _pidx = cx.fw.phase_begin()
    fw, nc, S = cx.fw, cx.nc, cx.NT
    ops = _ops(cx)
    PE, ACT, DVE, POOL, nb = ops
    cx.rot = cx.banks[0:4]
    cx.bi = 0
    ob = cx.banks[4:8]
    NSP = S // TP
    NG = 128 // GC
    cst = cx.cst
    iota = cview(cx, C_IOTA, 128)
    with ExitStack() as st_:
        sb = lambda n, s, d: fw.sbx(st_, n, s, d)
        G = sb("G", [128, TP, 128], BF16)
        Ab = [sb("A%d" % i, [128, 32, 128], BF16) for i in range(2)]
        Bb = [sb("B%d" % i, [128, 32, 128], BF16) for i in range(2)]
        RTb = [sb("RTt%d" % i, [128, 3, TP], F32) for i in range(2)]
        x1T = sb("x1T", [128, 8, TP], BF16)
        x1 = [sb("x1_%d" % i, [128, D], F32) for i in range(2)]
        UTb = [sb("UTb%d" % i, [128, GC, 1024], BF16) for i in range(3)]
        Vb = [sb("Vb%d" % i, [128, GC, 1024], BF16) for i in range(3)]
        sq = [sb("sq%d" % i, [128, TP], F32) for i in range(2)]
        uu = [sb("uu%d" % i, [128, TP], F32) for i in range(2)]
        sg = [sb("sg%d" % i, [128, TP], F32) for i in range(2)]
        xg = [sb("xg%d" % i, [128, TP], F32) for i in range(2)]
        lnp = sb("lnp", [128, 2 * D], F32)
        xo = [sb("xo%d" % i, [128, D], F32) for i in range(2)]
        epsb = sb("epsb", [128, 1], F32)
        lnset = [(sb("stats", [128, 12], F32), sb("mv", [128, 2], F32), sb("rs", [128, 1], F32), sb("nmr", [128, 1], F32), epsb)
                 for i in range(2)]
        cx.lnp_r = lnp
        fw.dma(fw.sync, lnp.t[:], cx.lnp[l][:, 2 * D:4 * D], reads=[], writes=[lnp])
        POOL(lambda: nc.gpsimd.memset(epsb.t[:], LN_EPS), [], [epsb])
        UT16, V16 = cx.UT16[l], cx.V16[l]
        iob = sb("iob", [128, 128], BF16)
        DVE(lambda: nc.vector.tensor_copy(iob.t[:], iota), [cst], [iob])

        def load_group(cg):
            slot = cg % 3
            fw.dma(fw.sync, UTb[slot].t[:], UT16[cg * GC:(cg + 1) * GC].rearrange("c p f -> p c f"), reads=[], writes=[UTb[slot]])
            fw.dma(fw.sync, Vb[slot].t[:], V16[cg * GC:(cg + 1) * GC].rearrange("c e d -> e c d"), reads=[], writes=[Vb[slot]])

        DEPTH = 3
        a2 = [sb("a2p_%d" % i, [128, TP], BF16) for i in range(DEPTH + 1)]
        A_r = [[fw.res("A%d_%d" % (i, t)) for t in range(32)] for i in range(2)]
        B_r = [[fw.res("B%d_%d" % (i, t)) for t in range(32)] for i in range(2)]
        G_r = [fw.res("G%d" % i) for i in range(TP // 4)]
        NCH = 128
        def prologue(sp):
            t0 = sp * TP
            RTt = RTb[sp % 2]
            fw.dma(fw.sync, RTt.t[:], cx.Rsc[:, :, t0:t0 + TP], reads=[], writes=[RTt])
            fw.dma(fw.sync, x1T.t[:], cx.X1T[:, :, t0:t0 + TP], reads=[], writes=[x1T])
            load_group(0)
            load_group(1)
            for sbk in range(TP // 32):
                A_, B_ = Ab[sbk % 2], Bb[sbk % 2]
                Ar, Br = A_r[sbk % 2], B_r[sbk % 2]
                for tt in range(32):
                    t = sbk * 32 + tt
                    DVE(lambda A_=A_, tt=tt, t=t, RTt=RTt: nc.vector.tensor_scalar(A_.t[:, tt, :], iob.t[:], RTt.t[:, 0, t:t + 1], RTt.t[:, 2, t:t + 1],
                                                                          ALU.is_equal, ALU.mult), [iob, RTt], [Ar[tt]])
                    DVE(lambda B_=B_, tt=tt, t=t, RTt=RTt: nc.vector.tensor_scalar(B_.t[:, tt, :], iob.t[:], RTt.t[:, 1, t:t + 1], None,
                                                                          ALU.is_equal), [iob, RTt], [Br[tt]])
                for q4 in range(8):
                    gb_ = nb()
                    for i in range(4):
                        tt = q4 * 4 + i
                        PE(lambda gb_=gb_, i=i, tt=tt, A_=A_, B_=B_: nc.tensor.matmul(gb_.t[:, i * 128:(i + 1) * 128], B_.t[:, tt, :], A_.t[:, tt, :],
                                                                                      start=True, stop=True), [Ar[tt], Br[tt]], [gb_])
                    tg = sbk * 32 + q4 * 4
                    ACT(lambda gb_=gb_, tg=tg: nc.scalar.copy(G.t[:, tg:tg + 4, :], gb_.t[:, :].rearrange("p (t n) -> p t n", t=4)),
                        [gb_], [G_r[tg // 4]])

        prologue(0)
        for sp in range(NSP):
            t0 = sp * TP

            def stage_h(c):
                slot = (c // GC) % 3
                cc = c % GC
                i2 = c % 2
                ia = c % (DEPTH + 1)
                hb = nb()
                for dk in range(8):
                    PE(lambda hb=hb, dk=dk, cc=cc, slot=slot: nc.tensor.matmul(hb.t[:, 0:TP], UTb[slot].t[:, cc, dk * 128:(dk + 1) * 128], x1T.t[:, dk, :],
                                                                               start=(dk == 0), stop=(dk == 7)), [UTb[slot], x1T], [hb])
                ACT(lambda hb=hb, i2=i2: nc.scalar.activation(sq[i2].t[:], hb.t[:, 0:TP], AF.Square, scale=SQK), [hb], [sq[i2]])
                DVE(lambda hb=hb, i2=i2: nc.vector.scalar_tensor_tensor(uu[i2].t[:], sq[i2].t[:], 1.0, hb.t[:, 0:TP], ALU.add, ALU.mult),
                    [sq[i2], hb], [uu[i2]])
                ACT(lambda i2=i2: nc.scalar.activation(sg[i2].t[:], uu[i2].t[:], AF.Sigmoid, scale=GELU_S), [uu[i2]], [sg[i2]])
                DVE(lambda hb=hb, i2=i2, c=c: nc.vector.tensor_tensor(xg[i2].t[:], hb.t[:, 0:TP], G.t[:, :, c], ALU.mult), [hb] + G_r, [xg[i2]])
                POOL(lambda i2=i2, ia=ia: nc.gpsimd.tensor_tensor(a2[ia].t[:], sg[i2].t[:], xg[i2].t[:], ALU.mult), [sg[i2], xg[i2]], [a2[ia]])

            def stage_o(c):
                slot = (c // GC) % 3
                cc = c % GC
                ia = c % (DEPTH + 1)
                for j in range(2):
                    for hh in range(2):
                        o = ob[j * 2 + hh]
                        PE(lambda o=o, j=j, hh=hh, ia=ia, cc=cc, slot=slot, c=c: nc.tensor.matmul(
                            o.t[:, :], a2[ia].t[:, j * 128:(j + 1) * 128], Vb[slot].t[:, cc, hh * 512:(hh + 1) * 512],
                            start=(c == 0), stop=(c == NCH - 1)), [a2[ia], Vb[slot]], [o])

            for c in range(DEPTH):
                stage_h(c)
            for c in range(NCH):
                if c % GC == 0 and c // GC + 2 < NG:
                    load_group(c // GC + 2)
                if c + DEPTH < NCH:
                    stage_h(c + DEPTH)
                stage_o(c)
            if sp + 1 < NSP:
                prologue(sp + 1)

            def ln2_gen(j):
                xi, xo_ = x1[j], xo[j]
                fw.dma(fw.sync, xi.t[:], cx.X1[t0 + j * 128:t0 + (j + 1) * 128, :], reads=[], writes=[xi])
                for hh in range(2):
                    o = ob[j * 2 + hh]
                    DVE(lambda o=o, hh=hh: nc.vector.scalar_tensor_tensor(
                        xi.t[:, hh * 512:(hh + 1) * 512], xi.t[:, hh * 512:(hh + 1) * 512], ALPHA, o.t[:, :],
                        ALU.mult, ALU.add), [xi, o], [xi])
                yield
                yield from layer_norm_tile(cx, ops, xi, lnp.t[:, 0:D], lnp.t[:, D:2 * D], xo_, lnset[j])
                fw.dma(fw.sync, xdst[t0 + j * 128:t0 + (j + 1) * 128, :], xo_.t[:], reads=[xo_], writes=[],
                       final=(xdst is cx.out))
                yield
            lockstep([ln2_gen(0), ln2_gen(1)])
        fw.barrier()
        fw.phase_end(_pidx)


SEQ = 8192
NLAYER = 2
_CACHE = {}


def kernel(**inputs):
    x = np.asarray(inputs["x"], dtype=np.float32)
    shared = prep_shared(inputs, NLAYER)
    w_in_r = shared.pop("w_in_r")
    if "nc" not in _CACHE:
        _CACHE["nc"] = build_program(SEQ, NLAYER)[0]
    nc = _CACHE["nc"]
    in_maps = []
    for c in range(8):
        m = dict(shared)
        m["x"] = np.ascontiguousarray(x[c // 2])
        m["w_in"] = w_in_r[c % 2]
        m["x_own"] = np.ascontiguousarray(x[c // 2, (c % 2) * (SEQ // 2):(c % 2 + 1) * (SEQ // 2)])
        in_maps.append(m)
    res = run_bass_kernel_spmd(nc, in_maps, core_ids=list(range(8)))
    outs = [np.asarray(res.results[c]["out"], dtype=np.float32) for c in range(8)]
    return np.stack([np.concatenate([outs[2 * b], outs[2 * b + 1]], axis=0) for b in range(4)])
```
